# Optimizing a Trainium2 kernel written in Bass

```python
import math
import jax, jax.numpy as jnp
from jax import lax
import numpy as np

D_MODEL = 2048
BATCH = 4
SEQ = 4096
DEPTH = 1

D_RNN = D_MODEL
RNN_BLOCKS = 16
RNN_BLOCK = D_RNN // RNN_BLOCKS
RNN_CONV = 4
LRU_C = 8.0
HEAD_DIM = 128
N_HEADS = D_MODEL // HEAD_DIM
N_KV_HEADS = 4
GROUP = N_HEADS // N_KV_HEADS
IDX_HEADS = 16
IDX_DIM = 64
TOPK_MAX = 256
Q_BLOCK = 128
ROPE_THETA = 500000.0
ROPE_FRAC = 4
D_FF = 256 * ((8 * D_MODEL // 3 + 255) // 256)
FFN_CONV = 3
LN_EPS = 1e-5
DEEPNORM_ALPHA = (2 * DEPTH) ** 0.25
DEEPNORM_BETA = (8 * DEPTH) ** -0.25

SPLITS = [D_RNN,
          D_RNN,
          N_HEADS * HEAD_DIM,
          N_KV_HEADS * HEAD_DIM,
          N_KV_HEADS * HEAD_DIM,
          IDX_HEADS * IDX_DIM,
          IDX_DIM,
          IDX_HEADS,
          D_MODEL,
          D_MODEL]
D_IN = sum(SPLITS)
SPLIT_POINTS = [int(v) for v in np.cumsum(SPLITS)[:-1]]

kernel_name = 'hybrid_rglru_dsa_convffn_deepnorm'


def layer_norm(x, g, b):
    xf = x.astype(jnp.float32)
    mu = jnp.mean(xf, axis=-1, keepdims=True)
    var = jnp.mean(jnp.square(xf - mu), axis=-1, keepdims=True)
    y = (xf - mu) * lax.rsqrt(var + LN_EPS)
    return (y * g.astype(jnp.float32) + b.astype(jnp.float32)).astype(x.dtype)


def rope_partial(x, pos):
    d = x.shape[-1]
    rd = d // ROPE_FRAC
    half = rd // 2
    inv = jnp.power(ROPE_THETA, -jnp.arange(half, dtype=jnp.float32) * 2.0 / rd)
    ang = pos.astype(jnp.float32)[:, None] * inv[None, :]
    cos = jnp.cos(ang)[:, None, :]
    sin = jnp.sin(ang)[:, None, :]
    xf = x.astype(jnp.float32)
    x1 = xf[..., :half]
    x2 = xf[..., half:rd]
    out = jnp.concatenate([x1 * cos - x2 * sin, x2 * cos + x1 * sin, xf[..., rd:]], axis=-1)
    return out.astype(x.dtype)


def causal_dwconv(x, w, b):
    width = w.shape[0]
    y = lax.conv_general_dilated(
        x, w[:, None, :].astype(x.dtype), window_strides=(1,), padding=[(width - 1, 0)],
        dimension_numbers=('NWC', 'WIO', 'NWC'), feature_group_count=x.shape[-1])
    return y + b


def rg_lru(x, w_a, b_a, w_i, b_i, lam):
    B, T, _ = x.shape
    xb = x.reshape(B, T, RNN_BLOCKS, RNN_BLOCK)
    r = jax.nn.sigmoid(jnp.einsum('btnc,ncd->btnd', xb, w_a) + b_a).reshape(B, T, D_RNN)
    i = jax.nn.sigmoid(jnp.einsum('btnc,ncd->btnd', xb, w_i) + b_i).reshape(B, T, D_RNN)
    log_a = -LRU_C * r.astype(jnp.float32) * jax.nn.softplus(-lam.astype(jnp.float32))
    a = jnp.exp(log_a)
    mult = jnp.sqrt(-jnp.expm1(2.0 * log_a))
    u = mult * (i * x).astype(jnp.float32)

    def combine(left, right):
        a1, b1 = left
        a2, b2 = right
        return a1 * a2, a2 * b1 + b2

    _, h = lax.associative_scan(combine, (a, u), axis=1)
    return h.astype(x.dtype)


def dsa_attention(q, k, v, qi, ki, wi, topk):
    B, T = q.shape[0], q.shape[1]
    nblk = T // Q_BLOCK
    kif = ki.astype(jnp.float32)
    key_pos = jnp.arange(T)

    def to_blocks(a):
        return a.reshape((B, nblk, Q_BLOCK) + a.shape[2:]).swapaxes(0, 1)

    def one_block(args):
        blk, qb, qib, wib = args
        qpos = blk * Q_BLOCK + jnp.arange(Q_BLOCK)
        causal = key_pos[None, :] <= qpos[:, None]
        logits = jnp.einsum('bqhd,bsd->bqhs', qib.astype(jnp.float32), kif) * (IDX_DIM ** -0.5)
        score = jnp.einsum('bqh,bqhs->bqs', wib.astype(jnp.float32), jax.nn.relu(logits))
        score = jnp.where(causal[None], score, -jnp.inf)
        _, idx = lax.top_k(score, topk)
        valid = idx <= qpos[None, :, None]
        kg = jax.vmap(lambda kk, ii: kk[ii])(k, idx)
        vg = jax.vmap(lambda vv, ii: vv[ii])(v, idx)
        qg = qb.reshape(B, Q_BLOCK, N_KV_HEADS, GROUP, HEAD_DIM)
        s = jnp.einsum('bqngd,bqknd->bqngk', qg, kg).astype(jnp.float32) * (HEAD_DIM ** -0.5)
        s = jnp.where(valid[:, :, None, None, :], s, -jnp.inf)
        p = jax.nn.softmax(s, axis=-1).astype(v.dtype)
        o = jnp.einsum('bqngk,bqknd->bqngd', p, vg)
        return o.reshape(B, Q_BLOCK, N_HEADS * HEAD_DIM)

    out = lax.map(one_block, (jnp.arange(nblk), to_blocks(q), to_blocks(qi), to_blocks(wi)))
    return out.swapaxes(0, 1).reshape(B, T, N_HEADS * HEAD_DIM)


def hybrid_layer(x, pos, topk, w_in, rnn_conv_w, rnn_conv_b, lru_wa, lru_ba, lru_wi, lru_bi,
                 lru_lambda, w_out, ln1_g, ln1_b, w_up, ffn_conv_w, ffn_conv_b, w_down, ln2_g, ln2_b):
    B, T, _ = x.shape
    h = jnp.einsum('btd,de->bte', x, w_in)
    xr, gr, q, k, v, qi, ki, wi, g_rnn, g_att = jnp.split(h, SPLIT_POINTS, axis=-1)
    xr = causal_dwconv(xr, rnn_conv_w, rnn_conv_b)
    y_rnn = rg_lru(xr, lru_wa, lru_ba, lru_wi, lru_bi, lru_lambda) * jax.nn.gelu(gr)
    q = rope_partial(q.reshape(B, T, N_HEADS, HEAD_DIM), pos)
    k = rope_partial(k.reshape(B, T, N_KV_HEADS, HEAD_DIM), pos)
    v = v.reshape(B, T, N_KV_HEADS, HEAD_DIM)
    qi = rope_partial(qi.reshape(B, T, IDX_HEADS, IDX_DIM), pos)
    ki = rope_partial(ki.reshape(B, T, 1, IDX_DIM), pos)[:, :, 0, :]
    wi = wi * (IDX_HEADS ** -0.5)
    y_att = dsa_attention(q, k, v, qi, ki, wi, topk)
    merged = jax.nn.sigmoid(g_rnn) * y_rnn + jax.nn.sigmoid(g_att) * y_att
    x = layer_norm(DEEPNORM_ALPHA * x + jnp.einsum('btd,de->bte', merged, w_out), ln1_g, ln1_b)
    u = causal_dwconv(jnp.einsum('btd,df->btf', x, w_up), ffn_conv_w, ffn_conv_b)
    gate, up = jnp.split(u, [D_FF], axis=-1)
    f = jnp.einsum('btf,fd->btd', jax.nn.silu(gate) * up, w_down)
    x = layer_norm(DEEPNORM_ALPHA * x + f, ln2_g, ln2_b)
    return x


def setup_inputs(seed: int = 0) -> dict:
    key = jax.random.key(seed)
    ks = jax.random.split(key, 20)
    nrm = jax.random.normal
    f32 = jnp.float32
    x = nrm(ks[0], (BATCH, SEQ, D_MODEL), f32)
    w_in = nrm(ks[1], (DEPTH, D_MODEL, D_IN), f32) * (D_MODEL ** -0.5)
    rnn_conv_w = nrm(ks[2], (DEPTH, RNN_CONV, D_RNN), f32) * (RNN_CONV ** -0.5)
    rnn_conv_b = 0.01 * nrm(ks[3], (DEPTH, D_RNN), f32)
    lru_wa = nrm(ks[4], (DEPTH, RNN_BLOCKS, RNN_BLOCK, RNN_BLOCK), f32) * (RNN_BLOCK ** -0.5)
    lru_ba = 0.01 * nrm(ks[5], (DEPTH, RNN_BLOCKS, RNN_BLOCK), f32)
    lru_wi = nrm(ks[6], (DEPTH, RNN_BLOCKS, RNN_BLOCK, RNN_BLOCK), f32) * (RNN_BLOCK ** -0.5)
    lru_bi = 0.01 * nrm(ks[7], (DEPTH, RNN_BLOCKS, RNN_BLOCK), f32)
    ac = jax.random.uniform(ks[8], (DEPTH, D_RNN), f32, minval=0.9, maxval=0.999)
    a0 = jnp.power(ac, 1.0 / LRU_C)
    lru_lambda = jnp.log(a0) - jnp.log1p(-a0)
    w_out = nrm(ks[9], (DEPTH, D_MODEL, D_MODEL), f32) * (D_MODEL ** -0.5) * DEEPNORM_BETA
    ln1_g = 1.0 + 0.01 * nrm(ks[10], (DEPTH, D_MODEL), f32)
    ln1_b = 0.01 * nrm(ks[11], (DEPTH, D_MODEL), f32)
    w_up = nrm(ks[12], (DEPTH, D_MODEL, 2 * D_FF), f32) * (D_MODEL ** -0.5)
    ffn_conv_w = nrm(ks[13], (DEPTH, FFN_CONV, 2 * D_FF), f32) * (FFN_CONV ** -0.5)
    ffn_conv_b = 0.01 * nrm(ks[14], (DEPTH, 2 * D_FF), f32)
    w_down = nrm(ks[15], (DEPTH, D_FF, D_MODEL), f32) * (D_FF ** -0.5) * DEEPNORM_BETA
    ln2_g = 1.0 + 0.01 * nrm(ks[16], (DEPTH, D_MODEL), f32)
    ln2_b = 0.01 * nrm(ks[17], (DEPTH, D_MODEL), f32)
    return {'x': x, 'w_in': w_in, 'rnn_conv_w': rnn_conv_w, 'rnn_conv_b': rnn_conv_b,
            'lru_wa': lru_wa, 'lru_ba': lru_ba, 'lru_wi': lru_wi, 'lru_bi': lru_bi,
            'lru_lambda': lru_lambda, 'w_out': w_out, 'ln1_g': ln1_g, 'ln1_b': ln1_b,
            'w_up': w_up, 'ffn_conv_w': ffn_conv_w, 'ffn_conv_b': ffn_conv_b, 'w_down': w_down,
            'ln2_g': ln2_g, 'ln2_b': ln2_b}


def reference(x, w_in, rnn_conv_w, rnn_conv_b, lru_wa, lru_ba, lru_wi, lru_bi, lru_lambda,
              w_out, ln1_g, ln1_b, w_up, ffn_conv_w, ffn_conv_b, w_down, ln2_g, ln2_b):
    T = x.shape[1]
    topk = min(TOPK_MAX, T // 4)
    pos = jnp.arange(T, dtype=jnp.int32)
    for l in range(DEPTH):
        x = hybrid_layer(x, pos, topk, w_in[l], rnn_conv_w[l], rnn_conv_b[l], lru_wa[l], lru_ba[l],
                         lru_wi[l], lru_bi[l], lru_lambda[l], w_out[l], ln1_g[l], ln1_b[l],
                         w_up[l], ffn_conv_w[l], ffn_conv_b[l], w_down[l], ln2_g[l], ln2_b[l])
    return x
```

```python
import numpy as np
import ml_dtypes
import concourse.bass as bass
import concourse.mybir as mybir
from concourse.bass_utils import run_bass_kernel_spmd

F32 = mybir.dt.float32
BF16 = mybir.dt.bfloat16
AF = mybir.ActivationFunctionType
ALU = mybir.AluOpType
ENGS = ("pe", "act", "dve", "pool", "sp")

D = 2048
T = 4096
NQT = 17
TQ = NQT * 128
DFF = 5632
NEG = -1.0e30
ALPHA = 2.0 ** 0.25
DEBUG = False


class _Op:
    __slots__ = ("eng", "fn", "waits", "signal", "sigval", "dsem")

    def __init__(self, eng, fn, dsem):
        self.eng = eng
        self.fn = fn
        self.waits = []
        self.signal = False
        self.sigval = None
        self.dsem = dsem


class Prog:
    def __init__(self, nc):
        self.nc = nc
        self.ops = {e: [] for e in ENGS}
        self.buf = {}
        self.waited = {e: {} for e in ENGS}
        self.dma_cnt = {}
        self.off = 16384
        self.maxoff = 0
        self._names = 0
        self.bar = {e: None for e in ENGS}

    def sb(self, shape, dtype, name=None):
        self._names += 1
        nm = (name or "sb") + f"_{self._names}"
        n = 1
        for s in shape[1:]:
            n *= s
        nbytes = n * (4 if dtype == F32 else 2)
        nbytes = (nbytes + 31) // 32 * 32
        t = self.nc.alloc_sbuf_tensor_at(nm, list(shape), dtype, offset=self.off)
        self.off += nbytes
        self.maxoff = max(self.maxoff, self.off)
        assert self.off <= 229000, (nm, self.off)
        return t

    def barrier(self):
        snap = ({e: len(self.ops[e]) - 1 for e in ENGS if e != "sp"}, dict(self.dma_cnt))
        for e in ENGS:
            self.bar[e] = snap

    def add(self, eng, fn, r=(), w=(), dsem=None):
        op = _Op(eng, fn, dsem)
        idx = len(self.ops[eng])
        if dsem is not None:
            c = self.dma_cnt.get(dsem, 0) + 16
            self.dma_cnt[dsem] = c
            me = ("D", dsem, c)
        else:
            me = ("C", eng, idx)
        deps = []
        if self.bar[eng] is not None:
            cs, ds = self.bar[eng]
            self.bar[eng] = None
            for e2, j in cs.items():
                if j >= 0 and e2 != eng:
                    deps.append((("C", e2, j), "RAW"))
            for k, v in ds.items():
                deps.append((("D", k, v), "RAW"))
        for k in r:
            st = self.buf.get(k)
            if st and st[0] is not None:
                deps.append((st[0], "RAW"))
        for k in w:
            st = self.buf.get(k)
            if st:
                if st[0] is not None:
                    deps.append((st[0], "WAW"))
                for rd in st[1]:
                    deps.append((rd, "WAR"))
        wt = self.waited[eng]
        for dep, kind in deps:
            if dep[0] == "C":
                if dep[1] == eng and dsem is None:
                    if eng == "pe" or kind != "RAW":
                        continue
                j = dep[2]
                if wt.get(dep[1], -1) >= j:
                    continue
                wt[dep[1]] = j
                self.ops[dep[1]][j].signal = True
                op.waits.append(dep)
            else:
                key = ("D", dep[1])
                if wt.get(key, 0) >= dep[2]:
                    continue
                wt[key] = dep[2]
                op.waits.append(dep)
        for k in r:
            st = self.buf.setdefault(k, [None, []])
            st[1].append(me)
            if len(st[1]) > 64:
                st[1] = st[1][-48:]
        for k in w:
            self.buf[k] = [me, []]
        self.ops[eng].append(op)
        return op

    def emit(self, final_waits=()):
        nc = self.nc
        csem = {e: nc.alloc_semaphore(f"c_{e}") for e in ENGS if e != "sp"}
        dsem = {k: nc.alloc_semaphore(f"d_{i}") for i, k in enumerate(self.dma_cnt)}
        self.nsem = len(csem) + len(dsem)
        for e in ENGS:
            c = 0
            for op in self.ops[e]:
                if op.signal:
                    c += 1
                    op.sigval = c
        ops = self.ops

        def run(e, engobj):
            for op in ops[e]:
                for dep in op.waits:
                    if dep[0] == "C":
                        engobj.wait_ge(csem[dep[1]], ops[dep[1]][dep[2]].sigval)
                    else:
                        engobj.wait_ge(dsem[dep[1]], dep[2])
                ins = op.fn(engobj)
                if op.dsem is not None:
                    ins.then_inc(dsem[op.dsem], 16)
                elif op.signal:
                    ins.then_inc(csem[e], 1)

        with nc.Block() as block:
            @block.sync
            def _(eng):
                run("sp", eng)
                for k in final_waits:
                    eng.wait_ge(dsem[k], self.dma_cnt[k])

            @block.tensor
            def _(eng):
                run("pe", eng)

            @block.scalar
            def _(eng):
                run("act", eng)

            @block.vector
            def _(eng):
                run("dve", eng)

            @block.gpsimd
            def _(eng):
                run("pool", eng)


class Ring:
    def __init__(self, P, name, n, shape, dtype):
        self.t = [P.sb(shape, dtype, f"{name}{i}") for i in range(n)]
        self.name = name
        self.i = 0
        self.n = n

    def next(self):
        k = self.i % self.n
        self.i += 1
        return self.t[k], f"{self.name}{k}"


def blocks(n, step=512):
    return [(a, min(a + step, n)) for a in range(0, n, step)]


def build_nc(debug=False):
    nc = bass.Bass("TRN2", target_bir_lowering=False)
    P = Prog(nc)

    def din(name, shape, dt=F32):
        return nc.dram_tensor(name, list(shape), dt, kind="ExternalInput").ap()

    def dscr(name, shape, dt=F32):
        kind = "ExternalOutput" if debug else "Internal"
        return nc.dram_tensor(name, list(shape), dt, kind=kind).ap()

    xtw = din("xtw", [D, T])
    xown = din("xown", [2048, D])
    xhalo = din("xhalo", [128, D])
    winb = din("winb", [97, 128, 16, 128])
    wout = din("wout", [D, D])
    wupb = din("wupb", [88, 128, 16, 128])
    wdown = din("wdown", [DFF, D])
    cw_d = din("cw", [128, 16, 4]); cb_d = din("cb", [128, 16])
    lba_d = din("lba", [128, 16]); lbi_d = din("lbi", [128, 16]); lam_d = din("lam", [128, 16])
    lwa_d = din("lwa", [128, 16 * 128]); lwi_d = din("lwi", [128, 16 * 128])
    fcw_d = din("fcw", [128, 88, 3]); fcb_d = din("fcb", [128, 88])
    ln1g_d = din("ln1g", [128, D]); ln1b_d = din("ln1b", [128, D])
    ln2g_d = din("ln2g", [128, D]); ln2b_d = din("ln2b", [128, D])
    tabq_d = din("tabq", [128, 32, 2, 16]); tabi_d = din("tabi", [128, 32, 2, 8])
    kbias_d = din("kbias", [128, 1]); tri_d = din("tri", [128, 128]); flag_d = din("flag", [128, 1])

    AT_d = dscr("AT_d", [D, TQ]); GT_d = dscr("GT_d", [D, TQ])
    QT_d = dscr("QT_d", [16, 128, TQ], BF16); QIT_d = dscr("QIT_d", [8, 128, TQ], BF16)
    KT_d = dscr("KT_d", [4, 128, T], BF16); V_d = dscr("V_d", [T, 512], BF16); KIT_d = dscr("KIT_d", [128, T], BF16)
    MT_d = dscr("MT_d", [D, TQ], BF16)
    X1_d = dscr("X1_d", [TQ, D]); X1T_d = dscr("X1T_d", [D, TQ], BF16); Y_d = dscr("Y_d", [2048, D])
    out_d = nc.dram_tensor("out", [2048, D], F32, kind="ExternalOutput").ap()

    ps = [nc.alloc_psum_tensor(f"psb{i}", [128, 512], F32) for i in range(8)]
    psi = [0]

    prange = [0, 8]

    def psum():
        lo, hi = prange
        k = lo + psi[0] % (hi - lo)
        psi[0] += 1
        return ps[k], f"ps{k}"

    pfix = {}

    def psum_fixed(name, lo, n):
        c = pfix.get(name, 0)
        pfix[name] = c + 1
        k = lo + c % n
        return ps[k], f"ps{k}"

    def load_small(dap, shape, name):
        t = P.sb(shape, F32, name)
        P.add("sp", lambda e: e.dma_start(out=t[:], in_=dap), w=[name], dsem="small")
        return t

    cw = load_small(cw_d, [128, 16, 4], "cw"); cb = load_small(cb_d, [128, 16], "cb")
    lba = load_small(lba_d, [128, 16], "lba"); lbi = load_small(lbi_d, [128, 16], "lbi")
    lam = load_small(lam_d, [128, 16], "lam")
    fcw = load_small(fcw_d, [128, 88, 3], "fcw"); fcb = load_small(fcb_d, [128, 88], "fcb")
    flag = load_small(flag_d, [128, 1], "flag")
    ident = P.sb([128, 128], BF16, "ident")
    ones = P.sb([128, 128], BF16, "ones")
    cvec = P.sb([128, 16], F32, "cvec")
    carry = P.sb([128, 16], F32, "carry")
    rawhalo = P.sb([128, 16, 3], F32, "rawhalo")
    hcar = P.sb([128, 4], F32, "hcar")
    wis = P.sb([128, NQT, 16], F32, "wis")
    ffhalo = P.sb([128, 88, 2], F32, "ffhalo")
    stg = Ring(P, "stg", 3, [128, 2048], F32)
    base_off = P.off
    lwab = P.sb([128, 16, 128], BF16, "lwab")
    lwib = P.sb([128, 16, 128], BF16, "lwib")
    tabq = load_small(tabq_d, [128, 32, 2, 16], "tabq"); tabi = load_small(tabi_d, [128, 32, 2, 8], "tabi")
    identf = P.sb([128, 128], F32, "identf")

    P.add("pool", lambda e: e.memset(identf[:], 1.0), w=["identf"])
    P.add("pool", lambda e: e.affine_select(out=identf[:], in_=identf[:], pattern=[[-1, 128]], compare_op=ALU.is_equal,
                                            fill=0.0, base=0, channel_multiplier=1), r=["identf"], w=["identf"])
    P.add("pool", lambda e: e.tensor_copy(out=ident[:], in_=identf[:]), r=["identf"], w=["ident"])
    P.add("pool", lambda e: e.memset(ones[:], 1.0), w=["ones"])
    P.add("act", lambda e: e.activation(out=cvec[:], in_=lam[:], func=AF.Exp, scale=-1.0), r=["lam"], w=["cvec"])
    P.add("act", lambda e: e.activation(out=cvec[:], in_=cvec[:], func=AF.Ln, bias=1.0), r=["cvec"], w=["cvec"])
    P.add("dve", lambda e: e.tensor_scalar(out=cvec[:], in0=cvec[:], scalar1=-8.0, scalar2=None, op0=ALU.mult), r=["cvec"], w=["cvec"])
    for (src, dst, nm) in ((lwa_d, lwab, "lwab"), (lwi_d, lwib, "lwib")):
        st, sk = stg.next()
        P.add("sp", lambda e, st=st, src=src: e.dma_start(out=st[:], in_=src), w=[sk], dsem=sk)
        P.add("pool", lambda e, st=st, dst=dst: e.tensor_copy(out=dst[:].rearrange("p k c -> p (k c)"), in_=st[:]), r=[sk], w=[nm])

    P.barrier()

    def load_piece(src_ap, dst_ap, dst_keys):
        st, sk = stg.next()
        shp = list(src_ap.shape)
        if len(shp) == 3:
            sv = st[:].rearrange("p (k c) -> p k c", k=shp[1])
        else:
            sv = st[:, 0:shp[1]]
        P.add("sp", lambda e: e.dma_start(out=sv, in_=src_ap), w=[sk], dsem=sk)
        P.add(cast_eng[0], lambda e: e.tensor_copy(out=dst_ap, in_=sv), r=[sk], w=dst_keys)

    cast_eng = ["dve"]

    def run_streams(streams):
        live = [[g, n] for g, n in streams]
        while live:
            for it in list(live):
                for _ in range(it[1]):
                    try:
                        next(it[0])
                    except StopIteration:
                        live.remove(it)
                        break

    xtb = P.sb([128, 16, TQ], BF16, "xtb")
    raw = P.sb([128, 3 + TQ], F32, "raw")
    xc = P.sb([128, TQ], F32, "xc")
    rr = P.sb([128, TQ], F32, "rr")
    ii = P.sb([128, TQ], F32, "ii")
    aa = P.sb([128, TQ], F32, "aa")
    xcb = P.sb([128, TQ], BF16, "xcb")
    wfm = Ring(P, "wfm", 5, [128, 16, 128], BF16)
    wtm = Ring(P, "wtm", 1, [128, 16, 512], BF16)
    tmo = Ring(P, "tmo", 2, [128, 512], BF16)
    tmt = Ring(P, "tmt", 2, [128, 512], BF16)
    rtmp = Ring(P, "rtmp", 2, [128, 4 * 16 * 2], F32)

    def fm_matmul(wt, wk, c0, c1, pt):
        for kc in range(16):
            P.add("pe", lambda e, kc=kc: e.matmul(pt[:, 0:c1 - c0], lhsT=wt[:, kc, :], rhs=xtb[:, kc, c0:c1],
                                                  start=(kc == 0), stop=(kc == 15)), r=[wk, "xtb"], w=[pt_key[0]])

    pt_key = [None]

    def rope_epilogue(pt, pk, tile_w, nh, hd, half, tab, ob, ok):
        pv = pt[:, 0:nh * hd].rearrange("p (h d) -> p h d", h=nh)
        ov = ob[:, 0:nh * hd].rearrange("p (h d) -> p h d", h=nh)
        cos = tab[:, tile_w, 0, :].unsqueeze(1).to_broadcast([128, nh, half])
        sin = tab[:, tile_w, 1, :].unsqueeze(1).to_broadcast([128, nh, half])
        tt, tk = rtmp.next()
        t1 = tt[:, 0:nh * half].rearrange("p (h d) -> p h d", h=nh)
        t2 = tt[:, nh * half:2 * nh * half].rearrange("p (h d) -> p h d", h=nh)
        x1 = pv[:, :, 0:half]
        x2 = pv[:, :, half:2 * half]
        P.add("dve", lambda e: e.tensor_tensor(out=t1, in0=x1, in1=cos, op=ALU.mult), r=[pk, tab_key(tab)], w=[tk + "a"])
        P.add("dve", lambda e: e.tensor_tensor(out=t2, in0=x2, in1=sin, op=ALU.mult), r=[pk, tab_key(tab)], w=[tk + "b"])
        P.add("dve", lambda e: e.tensor_tensor(out=ov[:, :, 0:half], in0=t1, in1=t2, op=ALU.subtract), r=[tk + "a", tk + "b"], w=[ok])
        tt2, tk2 = rtmp.next()
        u1 = tt2[:, 0:nh * half].rearrange("p (h d) -> p h d", h=nh)
        u2 = tt2[:, nh * half:2 * nh * half].rearrange("p (h d) -> p h d", h=nh)
        P.add("dve", lambda e: e.tensor_tensor(out=u1, in0=x2, in1=cos, op=ALU.mult), r=[pk, tab_key(tab)], w=[tk2 + "a"])
        P.add("dve", lambda e: e.tensor_tensor(out=u2, in0=x1, in1=sin, op=ALU.mult), r=[pk, tab_key(tab)], w=[tk2 + "b"])
        P.add("dve", lambda e: e.tensor_tensor(out=ov[:, :, half:2 * half], in0=u1, in1=u2, op=ALU.add), r=[tk2 + "a", tk2 + "b"], w=[ok])
        P.add("act", lambda e: e.activation(out=ov[:, :, 2 * half:hd], in_=pv[:, :, 2 * half:hd], func=AF.Copy), r=[pk], w=[ok])

    def tab_key(tab):
        return "tabq" if tab is tabq else "tabi"

    def transpose_store(ob, ok, nblk, dst_fn):
        pt, pk = psum()
        ptb = pt[:].bitcast(BF16)
        for j in range(nblk):
            P.add("pe", lambda e, j=j: e.transpose(out=ptb[:, j * 128:(j + 1) * 128], in_=ob[:, j * 128:(j + 1) * 128],
                                                   identity=ident[:]), r=[ok, "ident"], w=[pk])
        tb, tk = tmt.next()
        P.add("act", lambda e: e.activation(out=tb[:, 0:nblk * 128], in_=ptb[:, 0:nblk * 128], func=AF.Copy), r=[pk], w=[tk])
        for j in range(nblk):
            P.add("sp", lambda e, j=j: e.dma_start(out=dst_fn(j), in_=tb[:, j * 128:(j + 1) * 128]), r=[tk], dsem=tk + "_st")

    def rglru_pass(passB):
        Tn = TQ if passB else 1920
        col0 = 1920 if passB else 0
        ncols = TQ if passB else 2048
        for kc in range(16):
            load_piece(xtw[kc * 128:(kc + 1) * 128, col0:col0 + 2048], xtb[:, kc, 0:2048], ["xtb"])
            if passB:
                load_piece(xtw[kc * 128:(kc + 1) * 128, col0 + 2048:col0 + TQ], xtb[:, kc, 2048:TQ], ["xtb"])
        NP = 4

        def rg_half(c, hf, t0, t1, wts):
            wt, wk, wt2, wk2, wt3, wk3, wt4, wk4 = wts
            n = t1 - t0
            K = lambda nm: f"{nm}{hf}"
            rawk = [f"raw{hf - 1}", f"raw{hf}"] if hf else ["raw0"]
            blks = [(t0 + a, t0 + b) for (a, b) in blocks(n)]
            if hf == 0:
                if passB:
                    P.add("dve", lambda e: e.tensor_copy(out=raw[:, 0:3], in_=rawhalo[:, c, :]), r=["rawhalo"], w=["raw0"])
                else:
                    P.add("dve", lambda e: e.memset(raw[:, 0:3], 0.0), w=["raw0"])
            for (c0, c1) in blks:
                pt, pk = psum()
                pt_key[0] = pk
                fm_matmul(wt, wk, c0, c1, pt)
                P.add("act", lambda e, pt=pt, c0=c0, c1=c1: e.activation(out=raw[:, 3 + c0:3 + c1], in_=pt[:, 0:c1 - c0], func=AF.Copy),
                      r=[pk], w=[K("raw")])
                yield
            P.add("dve", lambda e: e.tensor_scalar(out=xc[:, t0:t1], in0=raw[:, 3 + t0:3 + t1], scalar1=cw[:, c, 3:4], scalar2=cb[:, c:c + 1],
                                                   op0=ALU.mult, op1=ALU.add), r=rawk + ["cw", "cb"], w=[K("xc")])
            yield
            for j in (2, 1, 0):
                P.add("dve", lambda e, j=j: e.scalar_tensor_tensor(out=xc[:, t0:t1], in0=raw[:, j + t0:j + t1], scalar=cw[:, c, j:j + 1],
                                                                   in1=xc[:, t0:t1], op0=ALU.mult, op1=ALU.add),
                      r=rawk + ["cw", K("xc")], w=[K("xc")])
                yield
            P.add("act", lambda e: e.activation(out=xcb[:, t0:t1], in_=xc[:, t0:t1], func=AF.Copy), r=[K("xc")], w=[K("xcb")])
            yield
            for (c0, c1) in blks:
                for (wg, bg, dst, dk) in ((lwab, lba, rr, K("rr")), (lwib, lbi, ii, K("ii"))):
                    pt, pk = psum()
                    P.add("pe", lambda e, pt=pt, wg=wg, c0=c0, c1=c1: e.matmul(pt[:, 0:c1 - c0], lhsT=wg[:, c, :], rhs=xcb[:, c0:c1],
                                                                                start=True, stop=True), r=["lwab", "lwib", K("xcb")], w=[pk])
                    P.add("act", lambda e, pt=pt, bg=bg, dst=dst, c0=c0, c1=c1: e.activation(out=dst[:, c0:c1], in_=pt[:, 0:c1 - c0], func=AF.Sigmoid,
                                                                                              bias=bg[:, c:c + 1]), r=[pk, "lba", "lbi"], w=[dk])
                yield
            P.add("act", lambda e: e.activation(out=aa[:, t0:t1], in_=rr[:, t0:t1], func=AF.Exp, scale=cvec[:, c:c + 1]), r=[K("rr"), "cvec"], w=[K("aa")])
            yield
            P.add("pool", lambda e: e.tensor_tensor(out=rr[:, t0:t1], in0=aa[:, t0:t1], in1=aa[:, t0:t1], op=ALU.mult), r=[K("aa")], w=[K("rr")])
            yield
            P.add("act", lambda e: e.activation(out=rr[:, t0:t1], in_=rr[:, t0:t1], func=AF.Sqrt, scale=-1.0, bias=1.0), r=[K("rr")], w=[K("rr")])
            yield
            P.add("pool", lambda e: e.tensor_tensor(out=ii[:, t0:t1], in0=ii[:, t0:t1], in1=rr[:, t0:t1], op=ALU.mult), r=[K("ii"), K("rr")], w=[K("ii")])
            yield
            P.add("dve", lambda e: e.tensor_tensor(out=ii[:, t0:t1], in0=ii[:, t0:t1], in1=xc[:, t0:t1], op=ALU.mult), r=[K("ii"), K("xc")], w=[K("ii")])
            yield
            if passB and hf == 0:
                P.add("dve", lambda e: e.tensor_scalar(out=ii[:, 0:128], in0=ii[:, 0:128], scalar1=flag[:, 0:1], scalar2=None, op0=ALU.mult),
                      r=[K("ii"), "flag"], w=[K("ii")])
            if hf == 0:
                if passB:
                    P.add("dve", lambda e: e.tensor_tensor_scan(out=rr[:, t0:t1], data0=aa[:, t0:t1], data1=ii[:, t0:t1], initial=carry[:, c:c + 1],
                                                                op0=ALU.mult, op1=ALU.add), r=[K("aa"), K("ii"), "carry"], w=[K("rr")])
                else:
                    P.add("dve", lambda e: e.tensor_tensor_scan(out=rr[:, t0:t1], data0=aa[:, t0:t1], data1=ii[:, t0:t1], initial=0.0,
                                                                op0=ALU.mult, op1=ALU.add), r=[K("aa"), K("ii")], w=[K("rr")])
            else:
                P.add("dve", lambda e: e.tensor_tensor_scan(out=rr[:, t0:t1], data0=aa[:, t0:t1], data1=ii[:, t0:t1], initial=hcar[:, hf - 1:hf],
                                                            op0=ALU.mult, op1=ALU.add), r=[K("aa"), K("ii"), f"hcar{hf - 1}"], w=[K("rr")])
            if hf < NP - 1:
                P.add("dve", lambda e: e.tensor_copy(out=hcar[:, hf:hf + 1], in_=rr[:, t1 - 1:t1]), r=[K("rr")], w=[f"hcar{hf}"])
            elif not passB:
                P.add("dve", lambda e: e.tensor_scalar(out=carry[:, c:c + 1], in0=rr[:, t1 - 1:t1], scalar1=flag[:, 0:1], scalar2=None, op0=ALU.mult),
                      r=[K("rr"), "flag"], w=["carry"])
                P.add("dve", lambda e: e.tensor_copy(out=rawhalo[:, c, :], in_=raw[:, t1:t1 + 3]), r=[K("raw")], w=["rawhalo"])
            yield
            if not passB:
                return
            for (c0, c1) in blks:
                pt, pk = psum()
                pt_key[0] = pk
                fm_matmul(wt2, wk2, c0, c1, pt)
                P.add("act", lambda e, pt=pt, c0=c0, c1=c1: e.activation(out=xc[:, c0:c1], in_=pt[:, 0:c1 - c0], func=AF.Copy), r=[pk], w=[K("xc")])
                yield
            P.add("act", lambda e: e.activation(out=ii[:, t0:t1], in_=xc[:, t0:t1], func=AF.Square), r=[K("xc")], w=[K("ii")])
            yield
            P.add("pool", lambda e: e.tensor_scalar(out=ii[:, t0:t1], in0=ii[:, t0:t1], scalar1=0.044715, scalar2=1.0, op0=ALU.mult, op1=ALU.add), r=[K("ii")], w=[K("ii")])
            yield
            P.add("dve", lambda e: e.tensor_tensor(out=ii[:, t0:t1], in0=ii[:, t0:t1], in1=xc[:, t0:t1], op=ALU.mult), r=[K("ii"), K("xc")], w=[K("ii")])
            yield
            P.add("act", lambda e: e.activation(out=ii[:, t0:t1], in_=ii[:, t0:t1], func=AF.Sigmoid, scale=1.5957691216057308), r=[K("ii")], w=[K("ii")])
            yield
            P.add("pool", lambda e: e.tensor_tensor(out=ii[:, t0:t1], in0=ii[:, t0:t1], in1=xc[:, t0:t1], op=ALU.mult), r=[K("ii"), K("xc")], w=[K("ii")])
            yield
            P.add("dve", lambda e: e.tensor_tensor(out=rr[:, t0:t1], in0=rr[:, t0:t1], in1=ii[:, t0:t1], op=ALU.mult), r=[K("ii"), K("rr")], w=[K("rr")])
            yield
            for (c0, c1) in blks:
                pt, pk = psum()
                pt_key[0] = pk
                fm_matmul(wt3, wk3, c0, c1, pt)
                P.add("act", lambda e, pt=pt, c0=c0, c1=c1: e.activation(out=aa[:, c0:c1], in_=pt[:, 0:c1 - c0], func=AF.Sigmoid), r=[pk], w=[K("aa")])
                yield
            P.add("dve", lambda e: e.tensor_tensor(out=rr[:, t0:t1], in0=rr[:, t0:t1], in1=aa[:, t0:t1], op=ALU.mult), r=[K("aa"), K("rr")], w=[K("rr")])
            P.add("sp", lambda e: e.dma_start(out=AT_d[c * 128:(c + 1) * 128, t0:t1], in_=rr[:, t0:t1]), r=[K("rr")], dsem=K("st_rr"))
            yield
            for (c0, c1) in blks:
                pt, pk = psum()
                pt_key[0] = pk
                fm_matmul(wt4, wk4, c0, c1, pt)
                P.add("act", lambda e, pt=pt, c0=c0, c1=c1: e.activation(out=aa[:, c0:c1], in_=pt[:, 0:c1 - c0], func=AF.Sigmoid), r=[pk], w=[K("aa")])
                yield
            P.add("sp", lambda e: e.dma_start(out=GT_d[c * 128:(c + 1) * 128, t0:t1], in_=aa[:, t0:t1]), r=[K("aa")], dsem=K("st_aa"))
            yield

        def rg_stream():
            nxt = None
            th = Tn // NP
            for c in range(16):
                if nxt is None:
                    wt, wk = wfm.next()
                    load_piece(winb[c], wt[:], [wk])
                else:
                    wt, wk = nxt
                wt2 = wk2 = wt3 = wk3 = wt4 = wk4 = None
                if passB:
                    wt2, wk2 = wfm.next()
                    load_piece(winb[16 + c], wt2[:], [wk2])
                    wt3, wk3 = wfm.next()
                    load_piece(winb[65 + c], wt3[:], [wk3])
                    wt4, wk4 = wfm.next()
                    load_piece(winb[81 + c], wt4[:], [wk4])
                if c < 15:
                    nxt = wfm.next()
                    load_piece(winb[c + 1], nxt[0][:], [nxt[1]])
                wts = (wt, wk, wt2, wk2, wt3, wk3, wt4, wk4)
                bnd = [0, 512, 1024, 1536, Tn]
                live = [rg_half(c, k_, bnd[k_], bnd[k_ + 1], wts) for k_ in range(NP)]
                while live:
                    for g_ in list(live):
                        try:
                            next(g_)
                        except StopIteration:
                            live.remove(g_)
                    yield

        ntile = NQT if passB else 16
        wtile0 = 15 if passB else 0

        def tm_group(b0, nblk, epilogue):
            wt, wk = wtm.next()
            for j in range(nblk):
                load_piece(winb[b0 + j], wt[:, :, j * 128:(j + 1) * 128], [wk])
            for ti in range(ntile):
                pt, pk = psum()
                for kc in range(16):
                    P.add("pe", lambda e, kc=kc, pt=pt, ti=ti: e.matmul(pt[:, 0:nblk * 128], lhsT=xtb[:, kc, ti * 128:(ti + 1) * 128],
                                                                        rhs=wt[:, kc, 0:nblk * 128], start=(kc == 0), stop=(kc == 15)),
                          r=[wk, "xtb"], w=[pk])
                epilogue(ti, pt, pk)
                yield

        def ep_k(ti, pt, pk):
            ob, ok = tmo.next()
            rope_epilogue(pt, pk, wtile0 + ti, 4, 128, 16, tabq, ob, ok)
            wt_ = wtile0 + ti
            transpose_store(ob, ok, 4, lambda j: KT_d[j, :, wt_ * 128:(wt_ + 1) * 128])

        def ep_v(ti, pt, pk):
            ob, ok = tmo.next()
            P.add("act", lambda e: e.activation(out=ob[:], in_=pt[:], func=AF.Copy), r=[pk], w=[ok])
            wt_ = wtile0 + ti
            P.add("sp", lambda e: e.dma_start(out=V_d[wt_ * 128:(wt_ + 1) * 128, :], in_=ob[:]), r=[ok], dsem=ok + "_st")

        def ep_kiwi(ti, pt, pk):
            ob, ok = tmo.next()
            rope_epilogue(pt, pk, wtile0 + ti, 1, 64, 8, tabi, ob, ok)
            P.add("pool", lambda e: e.tensor_copy(out=ob[:, 64:128], in_=ob[:, 0:64]), r=[ok], w=[ok])
            wt_ = wtile0 + ti
            transpose_store(ob, ok, 1, lambda j: KIT_d[:, wt_ * 128:(wt_ + 1) * 128])
            if passB:
                P.add("dve", lambda e: e.tensor_scalar(out=wis[:, ti, :], in0=pt[:, 64:80], scalar1=0.25 * 0.125, scalar2=None, op0=ALU.mult),
                      r=[pk], w=["wis"])

        def ep_q(g):
            def f(ti, pt, pk):
                ob, ok = tmo.next()
                rope_epilogue(pt, pk, wtile0 + ti, 4, 128, 16, tabq, ob, ok)
                transpose_store(ob, ok, 4, lambda j: QT_d[g * 4 + j, :, ti * 128:(ti + 1) * 128])
            return f

        def ep_qi(g):
            def f(ti, pt, pk):
                ob, ok = tmo.next()
                rope_epilogue(pt, pk, wtile0 + ti, 8, 64, 8, tabi, ob, ok)
                transpose_store(ob, ok, 4, lambda j: QIT_d[g * 4 + j, :, ti * 128:(ti + 1) * 128])
            return f

        def tm_stream():
            yield from tm_group(48, 4, ep_k)
            yield from tm_group(52, 4, ep_v)
            yield from tm_group(64, 1, ep_kiwi)
            if passB:
                for g in range(4):
                    yield from tm_group(32 + 4 * g, 4, ep_q(g))
                for g in range(2):
                    yield from tm_group(56 + 4 * g, 4, ep_qi(g))

        run_streams([(rg_stream(), 2 if passB else 3), (tm_stream(), 1)])

    rglru_pass(False)
    rglru_pass(True)
    P.barrier()

    P.off = base_off
    KT = P.sb([128, 4, T], BF16, "KT")
    Vs = P.sb([128, 32, 512], BF16, "Vs")
    kiT = P.sb([128, T], BF16, "kiT")
    kbias = P.sb([128, 1], F32, "kbias")
    tri = load_small(tri_d, [128, 128], "tri")
    acc2 = [P.sb([128, T], F32, "acc0"), P.sb([128, T], F32, "acc1")]
    maskb = P.sb([128, T], BF16, "maskb")
    maskT2 = [P.sb([128, 32, 128], BF16, "maskT0"), P.sb([128, 32, 128], BF16, "maskT1")]
    QTr = Ring(P, "QTr", 1, [128, 16, 128], BF16)
    QIr = Ring(P, "QIr", 2, [128, 8, 128], BF16)
    relr = Ring(P, "relr", 4, [128, 512], BF16)
    dgr = Ring(P, "dgr", 2, [128, 16, 128], BF16)
    PTr = Ring(P, "PTr", 4, [128, 512], BF16)
    ATr = Ring(P, "ATr", 1, [128, 16, 128], F32)
    GTr = Ring(P, "GTr", 1, [128, 16, 128], F32)
    MTr = Ring(P, "MTr", 1, [128, 16, 128], BF16)
    rdr = Ring(P, "rdr", 1, [128, 512], F32)
    yr = Ring(P, "yr", 1, [128, 512], F32)
    bis = P.sb([128, 4], F32, "bis")

    for n in range(4):
        P.add("sp", lambda e, n=n: e.dma_start(out=KT[:, n, :], in_=KT_d[n]), w=["KT"], dsem="ld_KT")
    for q in range(4):
        P.add("sp", lambda e, q=q: e.dma_start(out=Vs[:, q * 8:(q + 1) * 8, :],
                                               in_=V_d[q * 1024:(q + 1) * 1024, :].rearrange("(c p) d -> p c d", p=128)), w=["Vs"], dsem="ld_V")
    P.add("sp", lambda e: e.dma_start(out=kiT[:], in_=KIT_d), w=["kiT"], dsem="ld_ki")
    P.add("sp", lambda e: e.dma_start(out=kbias[:], in_=kbias_d), w=["kbias"], dsem="ld_kb")

    prange[0] = 6
    NIT = 17
    B0 = 8.0
    SCALE = 128.0 ** -0.5

    def stageA(qt):
        W = (15 + qt + 1) * 128
        acc = acc2[qt % 2]
        ack = f"acc{qt % 2}"
        qiT, qik = QIr.next()
        P.add("sp", lambda e: e.dma_start(out=qiT[:], in_=QIT_d[:, :, qt * 128:(qt + 1) * 128].rearrange("h p t -> p h t")), w=[qik], dsem=qik)
        dg, dgk = dgr.next()
        for h in range(16):
            P.add("dve", lambda e, h=h: e.tensor_scalar(out=dg[:, h, :], in0=ident[:], scalar1=wis[:, qt, h:h + 1], scalar2=None, op0=ALU.mult),
                  r=["ident", "wis"], w=[dgk])
        yield
        pa, pak = ps[6], "ps6"
        for (c0, c1) in blocks(W):
            nn = c1 - c0
            pend = []

            def emit_acc(item, c0=c0, c1=c1, nn=nn):
                h_, rl_, rk_ = item
                P.add("pe", lambda e, h_=h_, rl_=rl_, nn=nn: e.matmul(pa[:, 0:nn], lhsT=dg[:, h_, :], rhs=rl_[:, 0:nn], start=(h_ == 0), stop=(h_ == 15)),
                      r=[dgk, rk_], w=[pak])

            for h in range(16):
                pt, pk = psum_fixed("pA", 4, 2)
                pb = (h % 2) * 64
                P.add("pe", lambda e, pt=pt, h=h, pb=pb, c0=c0, c1=c1: e.matmul(pt[:, 0:c1 - c0], lhsT=qiT[pb:pb + 64, h // 2, :],
                                                                                rhs=kiT[pb:pb + 64, c0:c1], start=True, stop=True),
                      r=[qik, "kiT"], w=[pk])
                rl, rk = relr.next()
                P.add("act", lambda e, pt=pt, rl=rl, nn=nn: e.activation(out=rl[:, 0:nn], in_=pt[:, 0:nn], func=AF.Relu), r=[pk], w=[rk])
                pend.append((h, rl, rk))
                if len(pend) > 1:
                    emit_acc(pend.pop(0))
                yield
            while pend:
                emit_acc(pend.pop(0))
            if c1 <= 2048:
                P.add("dve", lambda e, c0=c0, c1=c1, nn=nn: e.tensor_scalar(out=acc[:, c0:c1], in0=pa[:, 0:nn], scalar1=kbias[:, 0:1], scalar2=None, op0=ALU.add),
                      r=[pak, "kbias"], w=[ack])
            else:
                P.add("dve", lambda e, c0=c0, c1=c1, nn=nn: e.tensor_copy(out=acc[:, c0:c1], in_=pa[:, 0:nn]), r=[pak], w=[ack])
            yield
        P.add("dve", lambda e: e.tensor_tensor(out=acc[:, W - 128:W], in0=acc[:, W - 128:W], in1=tri[:], op=ALU.add), r=[ack, "tri"], w=[ack])
        yield

    def stageB(qt):
        W = (15 + qt + 1) * 128
        nch = W // 128
        acc = acc2[qt % 2]
        ack = f"acc{qt % 2}"
        maskT = maskT2[qt % 2]
        mtk_ = f"maskT{qt % 2}"
        P.add("dve", lambda e: e.memset(bis[:, 0:1], 0.0), w=["bis"])
        for it in range(NIT):
            wk_ = B0 / (2.0 ** it)
            P.add("dve", lambda e: e.tensor_scalar(out=maskb[:, 0:W], in0=acc[:, 0:W], scalar1=bis[:, 0:1], scalar2=0.0, op0=ALU.is_ge, op1=ALU.add,
                                                   accum_out=bis[:, 1:2]), r=[ack, "bis"], w=["maskb", "bis"])
            P.add("dve", lambda e, wk_=wk_: e.tensor_scalar(out=bis[:, 2:3], in0=bis[:, 1:2], scalar1=255.5, scalar2=wk_, op0=ALU.is_ge, op1=ALU.mult),
                  r=["bis"], w=["bis"])
            if it < NIT - 1:
                P.add("dve", lambda e, wk_=wk_: e.scalar_tensor_tensor(out=bis[:, 0:1], in0=bis[:, 2:3], scalar=-wk_ / 2.0, in1=bis[:, 0:1], op0=ALU.add, op1=ALU.add),
                      r=["bis"], w=["bis"])
            else:
                P.add("dve", lambda e, wk_=wk_: e.scalar_tensor_tensor(out=bis[:, 3:4], in0=bis[:, 2:3], scalar=-wk_, in1=bis[:, 0:1], op0=ALU.add, op1=ALU.add),
                      r=["bis"], w=["bis"])
            yield
        P.add("dve", lambda e: e.tensor_scalar(out=maskb[:, 0:W], in0=acc[:, 0:W], scalar1=bis[:, 3:4], scalar2=None, op0=ALU.is_ge), r=[ack, "bis"], w=["maskb"])
        yield
        for g0 in range(0, nch, 8):
            g1 = min(g0 + 8, nch)
            pt, pk = ps[7], "ps7"
            ptb = pt[:].bitcast(BF16)
            for sc in range(g0, g1):
                P.add("pe", lambda e, sc=sc, g0=g0, ptb=ptb: e.transpose(out=ptb[:, (sc - g0) * 128:(sc - g0 + 1) * 128], in_=maskb[:, sc * 128:(sc + 1) * 128],
                                                                         identity=ident[:]), r=["maskb", "ident"], w=[pk])
            P.add("act", lambda e, g0=g0, g1=g1, ptb=ptb: e.activation(out=maskT[:, g0:g1, :].rearrange("p c q -> p (c q)"), in_=ptb[:, 0:(g1 - g0) * 128], func=AF.Copy),
                  r=[pk], w=[mtk_])
            yield

    def stageD(qt):
        W = (15 + qt + 1) * 128
        nch = W // 128
        maskT = maskT2[qt % 2]
        mtk_ = f"maskT{qt % 2}"
        qT, qk = QTr.next()
        P.add("sp", lambda e: e.dma_start(out=qT[:], in_=QT_d[:, :, qt * 128:(qt + 1) * 128].rearrange("h p t -> p h t")), w=[qk], dsem=qk)
        aT, ak = ATr.next()
        gT, gk = GTr.next()
        P.add("sp", lambda e: e.dma_start(out=aT[:], in_=AT_d[:, qt * 128:(qt + 1) * 128].rearrange("(c p) t -> p c t", p=128)), w=[ak], dsem=ak)
        P.add("sp", lambda e: e.dma_start(out=gT[:], in_=GT_d[:, qt * 128:(qt + 1) * 128].rearrange("(c p) t -> p c t", p=128)), w=[gk], dsem=gk)
        mT, mk = MTr.next()
        for n in range(4):
            po, pok = psum_fixed("po", 0, 1)
            pd, pdk = psum_fixed("pd", 1, 1)
            pend = []

            def emit_pv(item, n=n, po=po, pd=pd, pok=pok, pdk=pdk):
                sc_, pT_, ptk_ = item
                P.add("pe", lambda e, sc_=sc_, pT_=pT_, n=n, po=po: e.matmul(po[:], lhsT=Vs[:, sc_, n * 128:(n + 1) * 128], rhs=pT_[:], start=(sc_ == 0), stop=(sc_ == nch - 1)),
                      r=["Vs", ptk_], w=[pok])
                P.add("pe", lambda e, sc_=sc_, pT_=pT_, pd=pd: e.matmul(pd[:], lhsT=ones[:], rhs=pT_[:], start=(sc_ == 0), stop=(sc_ == nch - 1)),
                      r=["ones", ptk_], w=[pdk])

            for sc in range(nch):
                pt, pk = psum_fixed("pS", 2, 2)
                P.add("pe", lambda e, pt=pt, sc=sc, n=n: e.matmul(pt[:], lhsT=KT[:, n, sc * 128:(sc + 1) * 128], rhs=qT[:, 4 * n:4 * n + 4, :],
                                                             start=True, stop=True), r=["KT", qk], w=[pk])
                pT, ptk = PTr.next()
                P.add("act", lambda e, pt=pt, pT=pT: e.activation(out=pT[:], in_=pt[:], func=AF.Exp, scale=SCALE), r=[pk], w=[ptk])
                P.add("dve" if sc % 3 == 2 else "pool", lambda e, pT=pT, sc=sc: e.tensor_tensor(out=pT[:].rearrange("p (g q) -> p g q", g=4), in0=pT[:].rearrange("p (g q) -> p g q", g=4),
                                                                      in1=maskT[:, sc, :].unsqueeze(1).to_broadcast([128, 4, 128]), op=ALU.mult),
                      r=[ptk, mtk_], w=[ptk])
                pend.append((sc, pT, ptk))
                if len(pend) > 2:
                    emit_pv(pend.pop(0))
                yield
            while pend:
                emit_pv(pend.pop(0))
            rd, rdk = rdr.next()
            P.add("dve", lambda e, rd=rd, pd=pd: e.tensor_scalar(out=rd[:], in0=pd[:], scalar1=1e-30, scalar2=None, op0=ALU.max), r=[pdk], w=[rdk])
            P.add("dve", lambda e, rd=rd: e.reciprocal(out=rd[:], in_=rd[:]), r=[rdk], w=[rdk])
            yy, yk = yr.next()
            P.add("dve", lambda e, yy=yy, po=po, rd=rd: e.tensor_tensor(out=yy[:], in0=po[:], in1=rd[:], op=ALU.mult), r=[pok, rdk], w=[yk])
            P.add("pool", lambda e, yy=yy, n=n: e.tensor_tensor(out=yy[:], in0=yy[:], in1=gT[:, 4 * n:4 * n + 4, :].rearrange("p c t -> p (c t)"), op=ALU.mult),
                  r=[yk, gk], w=[yk])
            P.add("pool", lambda e, yy=yy, n=n: e.tensor_tensor(out=mT[:, 4 * n:4 * n + 4, :].rearrange("p c t -> p (c t)"), in0=yy[:],
                                                                in1=aT[:, 4 * n:4 * n + 4, :].rearrange("p c t -> p (c t)"), op=ALU.add),
                  r=[yk, ak], w=[mk])
            yield
        P.add("sp", lambda e: e.dma_start(out=MT_d[:, qt * 128:(qt + 1) * 128].rearrange("(c p) t -> p c t", p=128), in_=mT[:]), r=[mk], dsem=mk + "_st")
        yield

    for m in range(NQT + 2):
        streams = []
        if m < NQT:
            streams.append((stageA(m), 5))
        if 0 <= m - 1 < NQT:
            streams.append((stageB(m - 1), 1))
        if 0 <= m - 2 < NQT:
            streams.append((stageD(m - 2), 5))
        run_streams(streams)
    P.barrier()

    P.off = base_off
    prange[0] = 0
    wob = P.sb([128, 16, D], BF16, "wob")
    lng = P.sb([128, D], F32, "lng")
    lnb = P.sb([128, D], F32, "lnb")
    mtr = Ring(P, "mtr", 2, [128, 16, 128], BF16)
    xr_ = Ring(P, "xres", 2, [128, D], F32)
    zr = Ring(P, "zr", 2, [128, D], F32)
    zb = Ring(P, "zb", 2, [128, D], BF16)
    x1t = Ring(P, "x1t", 2, [128, 16, 128], BF16)
    lnst = Ring(P, "lnst", 2, [128, 8], F32)
    sqj = P.sb([128, D], F32, "sqj")

    def layer_norm(z, zk, g, gkey, b, bkey):
        st, sk = lnst.next()
        P.add("dve", lambda e: e.reduce_sum(out=st[:, 0:1], in_=z[:], axis=mybir.AxisListType.X), r=[zk], w=[sk])
        yield
        P.add("dve", lambda e: e.tensor_scalar(out=st[:, 1:2], in0=st[:, 0:1], scalar1=-1.0 / D, scalar2=None, op0=ALU.mult), r=[sk], w=[sk])
        P.add("act", lambda e: e.activation(out=sqj[:], in_=z[:], func=AF.Square, bias=st[:, 1:2], accum_out=st[:, 2:3]), r=[zk, sk], w=["sqj", sk])
        yield
        P.add("dve", lambda e: e.tensor_scalar(out=st[:, 3:4], in0=st[:, 2:3], scalar1=1.0 / D, scalar2=1e-5, op0=ALU.mult, op1=ALU.add), r=[sk], w=[sk])
        P.add("act", lambda e: e.activation(out=st[:, 3:4], in_=st[:, 3:4], func=AF.Ln), r=[sk], w=[sk])
        P.add("act", lambda e: e.activation(out=st[:, 3:4], in_=st[:, 3:4], func=AF.Exp, scale=-0.5), r=[sk], w=[sk])
        yield
        P.add("dve", lambda e: e.tensor_scalar(out=z[:], in0=z[:], scalar1=st[:, 1:2], scalar2=st[:, 3:4], op0=ALU.add, op1=ALU.mult), r=[zk, sk], w=[zk])
        yield
        P.add("dve", lambda e: e.tensor_tensor(out=z[:], in0=z[:], in1=g[:], op=ALU.mult), r=[zk, gkey], w=[zk])
        yield
        P.add("dve", lambda e: e.tensor_tensor(out=z[:], in0=z[:], in1=b[:], op=ALU.add), r=[zk, bkey], w=[zk])


    def run_pool(gens, nactive):
        gens = list(gens)
        active = []
        while gens or active:
            while len(active) < nactive and gens:
                active.append(gens.pop(0))
            for g_ in list(active):
                try:
                    next(g_)
                except StopIteration:
                    active.remove(g_)

    for kc in range(16):
        load_piece(wout[kc * 128:(kc + 1) * 128, :], wob[:, kc, :], ["wob"])
    P.add("sp", lambda e: e.dma_start(out=lng[:], in_=ln1g_d), w=["lng"], dsem="ld_lng")
    P.add("sp", lambda e: e.dma_start(out=lnb[:], in_=ln1b_d), w=["lnb"], dsem="ld_lnb")
    def p3_tile(qt):
        mt, mtk = mtr.next()
        P.add("sp", lambda e, mt=mt, qt=qt: e.dma_start(out=mt[:], in_=MT_d[:, qt * 128:(qt + 1) * 128].rearrange("(c p) t -> p c t", p=128)), w=[mtk], dsem=mtk)
        xt_, xk = xr_.next()
        if qt == 0:
            P.add("sp", lambda e, xt_=xt_: e.dma_start(out=xt_[:], in_=xhalo), w=[xk], dsem=xk)
        else:
            P.add("sp", lambda e, xt_=xt_, qt=qt: e.dma_start(out=xt_[:], in_=xown[(qt - 1) * 128:qt * 128, :]), w=[xk], dsem=xk)
        z, zk = zr.next()
        for db in range(4):
            pt, pk = psum()
            for kc in range(16):
                P.add("pe", lambda e, pt=pt, kc=kc, mt=mt, db=db: e.matmul(pt[:], lhsT=mt[:, kc, :], rhs=wob[:, kc, db * 512:(db + 1) * 512],
                                                                           start=(kc == 0), stop=(kc == 15)), r=[mtk, "wob"], w=[pk])
            P.add("dve", lambda e, pt=pt, z=z, xt_=xt_, db=db: e.scalar_tensor_tensor(out=z[:, db * 512:(db + 1) * 512], in0=xt_[:, db * 512:(db + 1) * 512], scalar=ALPHA,
                                                                                      in1=pt[:], op0=ALU.mult, op1=ALU.add), r=[pk, xk], w=[zk])
            yield
        yield from layer_norm(z, zk, lng, "lng", lnb, "lnb")
        P.add("sp", lambda e, z=z, qt=qt: e.dma_start(out=X1_d[qt * 128:(qt + 1) * 128, :], in_=z[:]), r=[zk], dsem=zk + "_st")
        zbt, zbk = zb.next()
        P.add("act", lambda e, zbt=zbt, z=z: e.activation(out=zbt[:], in_=z[:], func=AF.Copy), r=[zk], w=[zbk])
        yield
        xo, xok = x1t.next()
        for h2 in range(2):
            pt, pk = psum()
            ptb = pt[:].bitcast(BF16)
            for j in range(8):
                cc = h2 * 8 + j
                P.add("pe", lambda e, ptb=ptb, j=j, cc=cc, zbt=zbt: e.transpose(out=ptb[:, j * 128:(j + 1) * 128], in_=zbt[:, cc * 128:(cc + 1) * 128], identity=ident[:]),
                      r=[zbk, "ident"], w=[pk])
            P.add("act", lambda e, ptb=ptb, xo=xo, h2=h2: e.activation(out=xo[:, h2 * 8:(h2 + 1) * 8, :].rearrange("p c t -> p (c t)"), in_=ptb[:, 0:1024], func=AF.Copy),
                  r=[pk], w=[xok])
            yield
        P.add("sp", lambda e, xo=xo, qt=qt: e.dma_start(out=X1T_d[:, qt * 128:(qt + 1) * 128].rearrange("(c p) t -> p c t", p=128), in_=xo[:]), r=[xok], dsem=xok + "_st")
    run_pool([p3_tile(qt) for qt in range(NQT)], 2)
    P.barrier()

    P.off = base_off
    NS = 1024
    x1T = P.sb([128, 16, NS + 2], BF16, "x1T")
    actT = P.sb([128, 44, NS], BF16, "actT")
    rawg = Ring(P, "rawg", 2, [128, NS + 2], F32)
    cvg = Ring(P, "cvg", 1, [128, NS], F32)
    cvu = Ring(P, "cvu", 1, [128, NS], F32)
    wup = Ring(P, "wup", 4, [128, 16, 128], BF16)
    wdn = Ring(P, "wdn", 4, [128, 4, 512], BF16)
    x1r = Ring(P, "x1r", 2, [128, 512], F32)
    yo = Ring(P, "yo", 2, [128, 512], F32)
    for sb_ in range(2):
        t0 = 128 + sb_ * NS
        P.add("sp", lambda e, t0=t0: e.dma_start(out=x1T[:], in_=X1T_d[:, t0 - 2:t0 + NS].rearrange("(c p) t -> p c t", p=128)), w=["x1T"], dsem="ld_x1T")
        order = [half_ * 44 + i for i in range(44) for half_ in range(2)]
        wq = []

        def up_prefetch():
            if order:
                fi_ = order.pop(0)
                wt_, wk_ = wup.next()
                load_piece(wupb[fi_], wt_[:], [wk_])
                wq.append((wt_, wk_))

        up_prefetch(); up_prefetch()
        for i in range(44):
            cvs = []
            for half_ in range(2):
                fi = half_ * 44 + i
                up_prefetch()
                wt, wk = wq.pop(0)
                rw, rwk = rawg.next()
                for (c0, c1) in [(0, 2), (2, 514), (514, 1026)]:
                    pt, pk = psum()
                    for kc in range(16):
                        P.add("pe", lambda e, pt=pt, wt=wt, kc=kc, c0=c0, c1=c1: e.matmul(pt[:, 0:c1 - c0], lhsT=wt[:, kc, :], rhs=x1T[:, kc, c0:c1],
                                                                                           start=(kc == 0), stop=(kc == 15)), r=[wk, "x1T"], w=[pk])
                    if c0 == 0 and sb_ == 0:
                        P.add("dve", lambda e, pt=pt, rw=rw: e.tensor_scalar(out=rw[:, 0:2], in0=pt[:, 0:2], scalar1=flag[:, 0:1], scalar2=None, op0=ALU.mult),
                              r=[pk, "flag"], w=[rwk])
                    else:
                        P.add("act", lambda e, pt=pt, rw=rw, c0=c0, c1=c1: e.activation(out=rw[:, c0:c1], in_=pt[:, 0:c1 - c0], func=AF.Copy), r=[pk], w=[rwk])
                cv, cvk = (cvg if half_ == 0 else cvu).next()
                P.add("dve", lambda e, cv=cv, rw=rw, fi=fi: e.tensor_scalar(out=cv[:], in0=rw[:, 2:2 + NS], scalar1=fcw[:, fi, 2:3], scalar2=fcb[:, fi:fi + 1],
                                                                            op0=ALU.mult, op1=ALU.add), r=[rwk, "fcw", "fcb"], w=[cvk])
                for j in (1, 0):
                    P.add("dve", lambda e, cv=cv, rw=rw, fi=fi, j=j: e.scalar_tensor_tensor(out=cv[:], in0=rw[:, j:j + NS], scalar=fcw[:, fi, j:j + 1], in1=cv[:],
                                                                                            op0=ALU.mult, op1=ALU.add), r=[rwk, "fcw", cvk], w=[cvk])
                cvs.append((cv, cvk))
            (cg, cgk), (cu, cuk) = cvs
            P.add("act", lambda e, cg=cg: e.activation(out=cg[:], in_=cg[:], func=AF.Silu), r=[cgk], w=[cgk])
            P.add("pool", lambda e, cg=cg, cu=cu, i=i: e.tensor_tensor(out=actT[:, i, :], in0=cg[:], in1=cu[:], op=ALU.mult), r=[cgk, cuk], w=["actT"])
        dorder = [(db, q4) for db in range(4) for q4 in range(11)]
        dq = []

        def dn_prefetch():
            if dorder:
                db_, q4_ = dorder.pop(0)
                wd_, wdk_ = wdn.next()
                load_piece(wdown[q4_ * 512:(q4_ + 1) * 512, db_ * 512:(db_ + 1) * 512].rearrange("(f p) d -> p f d", p=128), wd_[:], [wdk_])
                dq.append((wd_, wdk_))

        dn_prefetch(); dn_prefetch()
        for db in range(4):
            pts = [(ps[k], f"ps{k}") for k in range(8)]
            for q4 in range(11):
                dn_prefetch()
                wd, wdk = dq.pop(0)
                for tt_ in range(8):
                    pt, pk = pts[tt_]
                    for f4 in range(4):
                        fc = q4 * 4 + f4
                        P.add("pe", lambda e, pt=pt, wd=wd, f4=f4, fc=fc, tt_=tt_: e.matmul(pt[:], lhsT=actT[:, fc, tt_ * 128:(tt_ + 1) * 128], rhs=wd[:, f4, :],
                                                                                            start=(fc == 0), stop=(fc == 43)), r=["actT", wdk], w=[pk])
            for tt_ in range(8):
                pt, pk = pts[tt_]
                row0 = t0 + tt_ * 128
                xx, xxk = x1r.next()
                P.add("sp", lambda e, xx=xx, row0=row0, db=db: e.dma_start(out=xx[:], in_=X1_d[row0:row0 + 128, db * 512:(db + 1) * 512]), w=[xxk], dsem=xxk)
                y_, yk = yo.next()
                P.add("dve", lambda e, y_=y_, xx=xx, pt=pt: e.scalar_tensor_tensor(out=y_[:], in0=xx[:], scalar=ALPHA, in1=pt[:], op0=ALU.mult, op1=ALU.add),
                      r=[pk, xxk], w=[yk])
                P.add("sp", lambda e, y_=y_, row0=row0, db=db: e.dma_start(out=Y_d[row0 - 128:row0, db * 512:(db + 1) * 512], in_=y_[:]), r=[yk], dsem=yk + "_st")
    P.barrier()

    P.add("sp", lambda e: e.dma_start(out=lng[:], in_=ln2g_d), w=["lng"], dsem="ld_lng")
    P.add("sp", lambda e: e.dma_start(out=lnb[:], in_=ln2b_d), w=["lnb"], dsem="ld_lnb")
    outkeys = set()
    def p5_tile(tt_):
        z, zk = zr.next()
        P.add("sp", lambda e: e.dma_start(out=z[:], in_=Y_d[tt_ * 128:(tt_ + 1) * 128, :]), w=[zk], dsem=zk + "_ld")
        yield
        yield from layer_norm(z, zk, lng, "lng", lnb, "lnb")
        P.add("sp", lambda e: e.dma_start(out=out_d[tt_ * 128:(tt_ + 1) * 128, :], in_=z[:]), r=[zk], dsem=zk + "_st")
        outkeys.add(zk + "_st")
        yield

    run_pool([p5_tile(t_) for t_ in range(16)], 2)
    P.emit(final_waits=sorted(outkeys))
    return nc, P


def _blk(w):
    K, M = w.shape
    return np.ascontiguousarray(w.reshape(K // 128, 128, M // 128, 128).transpose(2, 1, 0, 3))


def _fm(v):
    return np.ascontiguousarray(v.reshape(-1, 128).T)


_NC_CACHE = {}


def kernel(x, w_in, rnn_conv_w, rnn_conv_b, lru_wa, lru_ba, lru_wi, lru_bi, lru_lambda, w_out, ln1_g, ln1_b,
           w_up, ffn_conv_w, ffn_conv_b, w_down, ln2_g, ln2_b, _debug=False):
    f32 = np.float32
    x = np.asarray(x, f32)
    wi_ = np.asarray(w_in, f32)[0]
    pad = np.zeros((D, 48), f32)
    w_in_p = np.concatenate([wi_[:, 0:8272], pad, wi_[:, 8272:]], axis=1)
    shared = {
        "winb": _blk(w_in_p),
        "wout": np.ascontiguousarray(np.asarray(w_out, f32)[0]),
        "wupb": _blk(np.asarray(w_up, f32)[0]),
        "wdown": np.ascontiguousarray(np.asarray(w_down, f32)[0]),
        "cw": np.ascontiguousarray(np.asarray(rnn_conv_w, f32)[0].reshape(4, 16, 128).transpose(2, 1, 0)),
        "cb": _fm(np.asarray(rnn_conv_b, f32)[0]),
        "lba": _fm(np.asarray(lru_ba, f32)[0].reshape(-1)),
        "lbi": _fm(np.asarray(lru_bi, f32)[0].reshape(-1)),
        "lam": _fm(np.asarray(lru_lambda, f32)[0]),
        "lwa": np.ascontiguousarray(np.asarray(lru_wa, f32)[0].transpose(1, 0, 2).reshape(128, 2048)),
        "lwi": np.ascontiguousarray(np.asarray(lru_wi, f32)[0].transpose(1, 0, 2).reshape(128, 2048)),
        "fcw": np.ascontiguousarray(np.asarray(ffn_conv_w, f32)[0].reshape(3, 88, 128).transpose(2, 1, 0)),
        "fcb": _fm(np.asarray(ffn_conv_b, f32)[0]),
        "ln1g": np.ascontiguousarray(np.broadcast_to(np.asarray(ln1_g, f32)[0], (128, D))),
        "ln1b": np.ascontiguousarray(np.broadcast_to(np.asarray(ln1_b, f32)[0], (128, D))),
        "ln2g": np.ascontiguousarray(np.broadcast_to(np.asarray(ln2_g, f32)[0], (128, D))),
        "ln2b": np.ascontiguousarray(np.broadcast_to(np.asarray(ln2_b, f32)[0], (128, D))),
        "tri": np.where(np.arange(128)[None, :] <= np.arange(128)[:, None], 0.0, NEG).astype(f32),
    }
    in_maps = []
    for core in range(8):
        b, j = core // 2, core % 2
        own0 = 2048 * j
        xtw = np.zeros((D, T), f32)
        if j == 0:
            xtw[:, 2048:] = x[b, 0:2048].T
        else:
            xtw[:, :] = x[b].T
        pos = (np.arange(T) + own0 - 2048).astype(f32)
        tabs = {}
        for nm, rd in (("tabq", 32), ("tabi", 16)):
            half = rd // 2
            inv = np.power(f32(500000.0), -np.arange(half, dtype=f32) * f32(2.0) / f32(rd)).astype(f32)
            ang = (pos[:, None] * inv[None, :]).astype(f32)
            tab = np.stack([np.cos(ang), np.sin(ang)], axis=1).astype(f32)
            tabs[nm] = np.ascontiguousarray(tab.reshape(32, 128, 2, half).transpose(1, 0, 2, 3))
        m = dict(shared)
        m.update(tabs)
        m["xtw"] = xtw
        m["xown"] = np.ascontiguousarray(x[b, own0:own0 + 2048])
        m["xhalo"] = np.ascontiguousarray(x[b, own0 - 128:own0]) if j == 1 else np.zeros((128, D), f32)
        m["kbias"] = np.full((128, 1), 0.0 if j == 1 else NEG, f32)
        m["flag"] = np.full((128, 1), float(j), f32)
        in_maps.append(m)
    key = bool(_debug)
    if key not in _NC_CACHE:
        _NC_CACHE[key] = build_nc(debug=key)[0]
    nc = _NC_CACHE[key]
    res = run_bass_kernel_spmd(nc, in_maps, core_ids=list(range(8)))
    out = np.zeros((4, T, D), f32)
    for core in range(8):
        b, j = core // 2, core % 2
        out[b, 2048 * j:2048 * (j + 1)] = res.results[core]["out"]
    if _debug:
        return out, res.results
    return out
```

```python
import numpy as np
import ml_dtypes
import concourse.bass as bass
import concourse.mybir as mybir
from concourse.bass_utils import run_bass_kernel_spmd

F32 = mybir.dt.float32
BF16 = mybir.dt.bfloat16
AF = mybir.ActivationFunctionType
ALU = mybir.AluOpType
ENGS = ("pe", "act", "dve", "pool", "sp")

D = 2048
T = 4096
NQT = 17
TQ = NQT * 128
DFF = 5632
NEG = -1.0e30
ALPHA = 2.0 ** 0.25
DEBUG = False


class _Op:
    __slots__ = ("eng", "fn", "waits", "signal", "sigval", "dsem")

    def __init__(self, eng, fn, dsem):
        self.eng = eng
        self.fn = fn
        self.waits = []
        self.signal = False
        self.sigval = None
        self.dsem = dsem


class Prog:
    def __init__(self, nc):
        self.nc = nc
        self.ops = {e: [] for e in ENGS}
        self.buf = {}
        self.waited = {e: {} for e in ENGS}
        self.dma_cnt = {}
        self.off = 16384
        self.maxoff = 0
        self._names = 0
        self.bar = {e: None for e in ENGS}

    def sb(self, shape, dtype, name=None):
        self._names += 1
        nm = (name or "sb") + f"_{self._names}"
        n = 1
        for s in shape[1:]:
            n *= s
        nbytes = n * (4 if dtype == F32 else 2)
        nbytes = (nbytes + 31) // 32 * 32
        t = self.nc.alloc_sbuf_tensor_at(nm, list(shape), dtype, offset=self.off)
        self.off += nbytes
        self.maxoff = max(self.maxoff, self.off)
        assert self.off <= 229000, (nm, self.off)
        return t

    def barrier(self):
        snap = ({e: len(self.ops[e]) - 1 for e in ENGS if e != "sp"}, dict(self.dma_cnt))
        for e in ENGS:
            self.bar[e] = snap

    def add(self, eng, fn, r=(), w=(), dsem=None):
        op = _Op(eng, fn, dsem)
        idx = len(self.ops[eng])
        if dsem is not None:
            c = self.dma_cnt.get(dsem, 0) + 16
            self.dma_cnt[dsem] = c
            me = ("D", dsem, c)
        else:
            me = ("C", eng, idx)
        deps = []
        if self.bar[eng] is not None:
            cs, ds = self.bar[eng]
            self.bar[eng] = None
            for e2, j in cs.items():
                if j >= 0 and e2 != eng:
                    deps.append((("C", e2, j), "RAW"))
            for k, v in ds.items():
                deps.append((("D", k, v), "RAW"))
        for k in r:
            st = self.buf.get(k)
            if st and st[0] is not None:
                deps.append((st[0], "RAW"))
        for k in w:
            st = self.buf.get(k)
            if st:
                if st[0] is not None:
                    deps.append((st[0], "WAW"))
                for rd in st[1]:
                    deps.append((rd, "WAR"))
        wt = self.waited[eng]
        for dep, kind in deps:
            if dep[0] == "C":
                if dep[1] == eng and dsem is None:
                    if eng == "pe" or kind != "RAW":
                        continue
                j = dep[2]
                if wt.get(dep[1], -1) >= j:
                    continue
                wt[dep[1]] = j
                self.ops[dep[1]][j].signal = True
                op.waits.append(dep)
            else:
                key = ("D", dep[1])
                if wt.get(key, 0) >= dep[2]:
                    continue
                wt[key] = dep[2]
                op.waits.append(dep)
        for k in r:
            st = self.buf.setdefault(k, [None, []])
            st[1].append(me)
            if len(st[1]) > 64:
                st[1] = st[1][-48:]
        for k in w:
            self.buf[k] = [me, []]
        self.ops[eng].append(op)
        return op

    def emit(self, final_waits=()):
        nc = self.nc
        csem = {e: nc.alloc_semaphore(f"c_{e}") for e in ENGS if e != "sp"}
        dsem = {k: nc.alloc_semaphore(f"d_{i}") for i, k in enumerate(self.dma_cnt)}
        self.nsem = len(csem) + len(dsem)
        for e in ENGS:
            c = 0
            for op in self.ops[e]:
                if op.signal:
                    c += 1
                    op.sigval = c
        ops = self.ops

        def run(e, engobj):
            for op in ops[e]:
                for dep in op.waits:
                    if dep[0] == "C":
                        engobj.wait_ge(csem[dep[1]], ops[dep[1]][dep[2]].sigval)
                    else:
                        engobj.wait_ge(dsem[dep[1]], dep[2])
                ins = op.fn(engobj)
                if op.dsem is not None:
                    ins.then_inc(dsem[op.dsem], 16)
                elif op.signal:
                    ins.then_inc(csem[e], 1)

        with nc.Block() as block:
            @block.sync
            def _(eng):
                run("sp", eng)
                for k in final_waits:
                    eng.wait_ge(dsem[k], self.dma_cnt[k])

            @block.tensor
            def _(eng):
                run("pe", eng)

            @block.scalar
            def _(eng):
                run("act", eng)

            @block.vector
            def _(eng):
                run("dve", eng)

            @block.gpsimd
            def _(eng):
                run("pool", eng)


class Ring:
    def __init__(self, P, name, n, shape, dtype):
        self.t = [P.sb(shape, dtype, f"{name}{i}") for i in range(n)]
        self.name = name
        self.i = 0
        self.n = n

    def next(self):
        k = self.i % self.n
        self.i += 1
        return self.t[k], f"{self.name}{k}"


def blocks(n, step=512):
    return [(a, min(a + step, n)) for a in range(0, n, step)]


def build_nc(debug=False):
    nc = bass.Bass("TRN2", target_bir_lowering=False)
    P = Prog(nc)

    def din(name, shape, dt=F32):
        return nc.dram_tensor(name, list(shape), dt, kind="ExternalInput").ap()

    def dscr(name, shape, dt=F32):
        kind = "ExternalOutput" if debug else "Internal"
        return nc.dram_tensor(name, list(shape), dt, kind=kind).ap()

    xtw = din("xtw", [D, T])
    xown = din("xown", [2048, D])
    xhalo = din("xhalo", [128, D])
    winb = din("winb", [97, 128, 16, 128])
    wout = din("wout", [D, D])
    wupb = din("wupb", [88, 128, 16, 128])
    wdown = din("wdown", [DFF, D])
    cw_d = din("cw", [128, 16, 4]); cb_d = din("cb", [128, 16])
    lba_d = din("lba", [128, 16]); lbi_d = din("lbi", [128, 16]); lam_d = din("lam", [128, 16])
    lwa_d = din("lwa", [128, 16 * 128]); lwi_d = din("lwi", [128, 16 * 128])
    fcw_d = din("fcw", [128, 88, 3]); fcb_d = din("fcb", [128, 88])
    ln1g_d = din("ln1g", [128, D]); ln1b_d = din("ln1b", [128, D])
    ln2g_d = din("ln2g", [128, D]); ln2b_d = din("ln2b", [128, D])
    tabq_d = din("tabq", [128, 32, 2, 16]); tabi_d = din("tabi", [128, 32, 2, 8])
    kbias_d = din("kbias", [128, 1]); tri_d = din("tri", [128, 128]); flag_d = din("flag", [128, 1])

    AT_d = dscr("AT_d", [D, TQ]); GT_d = dscr("GT_d", [D, TQ])
    QT_d = dscr("QT_d", [16, 128, TQ], BF16); QIT_d = dscr("QIT_d", [8, 128, TQ], BF16)
    KT_d = dscr("KT_d", [4, 128, T], BF16); V_d = dscr("V_d", [T, 512], BF16); KIT_d = dscr("KIT_d", [128, T], BF16)
    MT_d = dscr("MT_d", [D, TQ], BF16)
    X1_d = dscr("X1_d", [TQ, D]); X1T_d = dscr("X1T_d", [D, TQ], BF16); Y_d = dscr("Y_d", [2048, D])
    out_d = nc.dram_tensor("out", [2048, D], F32, kind="ExternalOutput").ap()

    ps = [nc.alloc_psum_tensor(f"psb{i}", [128, 512], F32) for i in range(8)]
    psi = [0]

    prange = [0, 8]

    def psum():
        lo, hi = prange
        k = lo + psi[0] % (hi - lo)
        psi[0] += 1
        return ps[k], f"ps{k}"

    pfix = {}

    def psum_fixed(name, lo, n):
        c = pfix.get(name, 0)
        pfix[name] = c + 1
        k = lo + c % n
        return ps[k], f"ps{k}"

    def load_small(dap, shape, name):
        t = P.sb(shape, F32, name)
        P.add("sp", lambda e: e.dma_start(out=t[:], in_=dap), w=[name], dsem="small")
        return t

    cw = load_small(cw_d, [128, 16, 4], "cw"); cb = load_small(cb_d, [128, 16], "cb")
    lba = load_small(lba_d, [128, 16], "lba"); lbi = load_small(lbi_d, [128, 16], "lbi")
    lam = load_small(lam_d, [128, 16], "lam")
    fcw = load_small(fcw_d, [128, 88, 3], "fcw"); fcb = load_small(fcb_d, [128, 88], "fcb")
    flag = load_small(flag_d, [128, 1], "flag")
    ident = P.sb([128, 128], BF16, "ident")
    ones = P.sb([128, 128], BF16, "ones")
    cvec = P.sb([128, 16], F32, "cvec")
    carry = P.sb([128, 16], F32, "carry")
    rawhalo = P.sb([128, 16, 3], F32, "rawhalo")
    hcar = P.sb([128, 4], F32, "hcar")
    wis = P.sb([128, NQT, 16], F32, "wis")
    ffhalo = P.sb([128, 88, 2], F32, "ffhalo")
    stg = Ring(P, "stg", 3, [128, 2048], F32)
    base_off = P.off
    lwab = P.sb([128, 16, 128], BF16, "lwab")
    lwib = P.sb([128, 16, 128], BF16, "lwib")
    tabq = load_small(tabq_d, [128, 32, 2, 16], "tabq"); tabi = load_small(tabi_d, [128, 32, 2, 8], "tabi")
    identf = P.sb([128, 128], F32, "identf")

    P.add("pool", lambda e: e.memset(identf[:], 1.0), w=["identf"])
    P.add("pool", lambda e: e.affine_select(out=identf[:], in_=identf[:], pattern=[[-1, 128]], compare_op=ALU.is_equal,
                                            fill=0.0, base=0, channel_multiplier=1), r=["identf"], w=["identf"])
    P.add("pool", lambda e: e.tensor_copy(out=ident[:], in_=identf[:]), r=["identf"], w=["ident"])
    P.add("pool", lambda e: e.memset(ones[:], 1.0), w=["ones"])
    P.add("act", lambda e: e.activation(out=cvec[:], in_=lam[:], func=AF.Exp, scale=-1.0), r=["lam"], w=["cvec"])
    P.add("act", lambda e: e.activation(out=cvec[:], in_=cvec[:], func=AF.Ln, bias=1.0), r=["cvec"], w=["cvec"])
    P.add("dve", lambda e: e.tensor_scalar(out=cvec[:], in0=cvec[:], scalar1=-8.0, scalar2=None, op0=ALU.mult), r=["cvec"], w=["cvec"])
    for (src, dst, nm) in ((lwa_d, lwab, "lwab"), (lwi_d, lwib, "lwib")):
        st, sk = stg.next()
        P.add("sp", lambda e, st=st, src=src: e.dma_start(out=st[:], in_=src), w=[sk], dsem=sk)
        P.add("pool", lambda e, st=st, dst=dst: e.tensor_copy(out=dst[:].rearrange("p k c -> p (k c)"), in_=st[:]), r=[sk], w=[nm])

    P.barrier()

    def load_piece(src_ap, dst_ap, dst_keys):
        st, sk = stg.next()
        shp = list(src_ap.shape)
        if len(shp) == 3:
            sv = st[:].rearrange("p (k c) -> p k c", k=shp[1])
        else:
            sv = st[:, 0:shp[1]]
        P.add("sp", lambda e: e.dma_start(out=sv, in_=src_ap), w=[sk], dsem=sk)
        P.add(cast_eng[0], lambda e: e.tensor_copy(out=dst_ap, in_=sv), r=[sk], w=dst_keys)

    cast_eng = ["dve"]

    def run_streams(streams):
        live = [[g, n] for g, n in streams]
        while live:
            for it in list(live):
                for _ in range(it[1]):
                    try:
                        next(it[0])
                    except StopIteration:
                        live.remove(it)
                        break

    xtb = P.sb([128, 16, TQ], BF16, "xtb")
    raw = P.sb([128, 3 + TQ], F32, "raw")
    xc = P.sb([128, TQ], F32, "xc")
    rr = P.sb([128, TQ], F32, "rr")
    ii = P.sb([128, TQ], F32, "ii")
    aa = P.sb([128, TQ], F32, "aa")
    xcb = P.sb([128, TQ], BF16, "xcb")
    wfm = Ring(P, "wfm", 5, [128, 16, 128], BF16)
    wtm = Ring(P, "wtm", 1, [128, 16, 512], BF16)
    tmo = Ring(P, "tmo", 2, [128, 512], BF16)
    tmt = Ring(P, "tmt", 2, [128, 512], BF16)
    rtmp = Ring(P, "rtmp", 2, [128, 4 * 16 * 2], F32)

    def fm_matmul(wt, wk, c0, c1, pt):
        for kc in range(16):
            P.add("pe", lambda e, kc=kc: e.matmul(pt[:, 0:c1 - c0], lhsT=wt[:, kc, :], rhs=xtb[:, kc, c0:c1],
                                                  start=(kc == 0), stop=(kc == 15)), r=[wk, "xtb"], w=[pt_key[0]])

    pt_key = [None]

    def rope_epilogue(pt, pk, tile_w, nh, hd, half, tab, ob, ok):
        pv = pt[:, 0:nh * hd].rearrange("p (h d) -> p h d", h=nh)
        ov = ob[:, 0:nh * hd].rearrange("p (h d) -> p h d", h=nh)
        cos = tab[:, tile_w, 0, :].unsqueeze(1).to_broadcast([128, nh, half])
        sin = tab[:, tile_w, 1, :].unsqueeze(1).to_broadcast([128, nh, half])
        tt, tk = rtmp.next()
        t1 = tt[:, 0:nh * half].rearrange("p (h d) -> p h d", h=nh)
        t2 = tt[:, nh * half:2 * nh * half].rearrange("p (h d) -> p h d", h=nh)
        x1 = pv[:, :, 0:half]
        x2 = pv[:, :, half:2 * half]
        P.add("dve", lambda e: e.tensor_tensor(out=t1, in0=x1, in1=cos, op=ALU.mult), r=[pk, tab_key(tab)], w=[tk + "a"])
        P.add("dve", lambda e: e.tensor_tensor(out=t2, in0=x2, in1=sin, op=ALU.mult), r=[pk, tab_key(tab)], w=[tk + "b"])
        P.add("dve", lambda e: e.tensor_tensor(out=ov[:, :, 0:half], in0=t1, in1=t2, op=ALU.subtract), r=[tk + "a", tk + "b"], w=[ok])
        tt2, tk2 = rtmp.next()
        u1 = tt2[:, 0:nh * half].rearrange("p (h d) -> p h d", h=nh)
        u2 = tt2[:, nh * half:2 * nh * half].rearrange("p (h d) -> p h d", h=nh)
        P.add("dve", lambda e: e.tensor_tensor(out=u1, in0=x2, in1=cos, op=ALU.mult), r=[pk, tab_key(tab)], w=[tk2 + "a"])
        P.add("dve", lambda e: e.tensor_tensor(out=u2, in0=x1, in1=sin, op=ALU.mult), r=[pk, tab_key(tab)], w=[tk2 + "b"])
        P.add("dve", lambda e: e.tensor_tensor(out=ov[:, :, half:2 * half], in0=u1, in1=u2, op=ALU.add), r=[tk2 + "a", tk2 + "b"], w=[ok])
        P.add("act", lambda e: e.activation(out=ov[:, :, 2 * half:hd], in_=pv[:, :, 2 * half:hd], func=AF.Copy), r=[pk], w=[ok])

    def tab_key(tab):
        return "tabq" if tab is tabq else "tabi"

    def transpose_store(ob, ok, nblk, dst_fn):
        pt, pk = psum()
        ptb = pt[:].bitcast(BF16)
        for j in range(nblk):
            P.add("pe", lambda e, j=j: e.transpose(out=ptb[:, j * 128:(j + 1) * 128], in_=ob[:, j * 128:(j + 1) * 128],
                                                   identity=ident[:]), r=[ok, "ident"], w=[pk])
        tb, tk = tmt.next()
        P.add("act", lambda e: e.activation(out=tb[:, 0:nblk * 128], in_=ptb[:, 0:nblk * 128], func=AF.Copy), r=[pk], w=[tk])
        for j in range(nblk):
            P.add("sp", lambda e, j=j: e.dma_start(out=dst_fn(j), in_=tb[:, j * 128:(j + 1) * 128]), r=[tk], dsem=tk + "_st")

    def rglru_pass(passB):
        Tn = TQ if passB else 1920
        col0 = 1920 if passB else 0
        ncols = TQ if passB else 2048
        for kc in range(16):
            load_piece(xtw[kc * 128:(kc + 1) * 128, col0:col0 + 2048], xtb[:, kc, 0:2048], ["xtb"])
            if passB:
                load_piece(xtw[kc * 128:(kc + 1) * 128, col0 + 2048:col0 + TQ], xtb[:, kc, 2048:TQ], ["xtb"])
        NP = 4

        def rg_half(c, hf, t0, t1, wts):
            wt, wk, wt2, wk2, wt3, wk3, wt4, wk4 = wts
            n = t1 - t0
            K = lambda nm: f"{nm}{hf}"
            rawk = [f"raw{hf - 1}", f"raw{hf}"] if hf else ["raw0"]
            blks = [(t0 + a, t0 + b) for (a, b) in blocks(n)]
            if hf == 0:
                if passB:
                    P.add("dve", lambda e: e.tensor_copy(out=raw[:, 0:3], in_=rawhalo[:, c, :]), r=["rawhalo"], w=["raw0"])
                else:
                    P.add("dve", lambda e: e.memset(raw[:, 0:3], 0.0), w=["raw0"])
            for (c0, c1) in blks:
                pt, pk = psum()
                pt_key[0] = pk
                fm_matmul(wt, wk, c0, c1, pt)
                P.add("act", lambda e, pt=pt, c0=c0, c1=c1: e.activation(out=raw[:, 3 + c0:3 + c1], in_=pt[:, 0:c1 - c0], func=AF.Copy),
                      r=[pk], w=[K("raw")])
                yield
            P.add("dve", lambda e: e.tensor_scalar(out=xc[:, t0:t1], in0=raw[:, 3 + t0:3 + t1], scalar1=cw[:, c, 3:4], scalar2=cb[:, c:c + 1],
                                                   op0=ALU.mult, op1=ALU.add), r=rawk + ["cw", "cb"], w=[K("xc")])
            yield
            for j in (2, 1, 0):
                P.add("dve", lambda e, j=j: e.scalar_tensor_tensor(out=xc[:, t0:t1], in0=raw[:, j + t0:j + t1], scalar=cw[:, c, j:j + 1],
                                                                   in1=xc[:, t0:t1], op0=ALU.mult, op1=ALU.add),
                      r=rawk + ["cw", K("xc")], w=[K("xc")])
                yield
            P.add("act", lambda e: e.activation(out=xcb[:, t0:t1], in_=xc[:, t0:t1], func=AF.Copy), r=[K("xc")], w=[K("xcb")])
            yield
            for (c0, c1) in blks:
                for (wg, bg, dst, dk) in ((lwab, lba, rr, K("rr")), (lwib, lbi, ii, K("ii"))):
                    pt, pk = psum()
                    P.add("pe", lambda e, pt=pt, wg=wg, c0=c0, c1=c1: e.matmul(pt[:, 0:c1 - c0], lhsT=wg[:, c, :], rhs=xcb[:, c0:c1],
                                                                                start=True, stop=True), r=["lwab", "lwib", K("xcb")], w=[pk])
                    P.add("act", lambda e, pt=pt, bg=bg, dst=dst, c0=c0, c1=c1: e.activation(out=dst[:, c0:c1], in_=pt[:, 0:c1 - c0], func=AF.Sigmoid,
                                                                                              bias=bg[:, c:c + 1]), r=[pk, "lba", "lbi"], w=[dk])
                yield
            P.add("act", lambda e: e.activation(out=aa[:, t0:t1], in_=rr[:, t0:t1], func=AF.Exp, scale=cvec[:, c:c + 1]), r=[K("rr"), "cvec"], w=[K("aa")])
            yield
            P.add("pool", lambda e: e.tensor_tensor(out=rr[:, t0:t1], in0=aa[:, t0:t1], in1=aa[:, t0:t1], op=ALU.mult), r=[K("aa")], w=[K("rr")])
            yield
            P.add("act", lambda e: e.activation(out=rr[:, t0:t1], in_=rr[:, t0:t1], func=AF.Sqrt, scale=-1.0, bias=1.0), r=[K("rr")], w=[K("rr")])
            yield
            P.add("pool", lambda e: e.tensor_tensor(out=ii[:, t0:t1], in0=ii[:, t0:t1], in1=rr[:, t0:t1], op=ALU.mult), r=[K("ii"), K("rr")], w=[K("ii")])
            yield
            P.add("dve", lambda e: e.tensor_tensor(out=ii[:, t0:t1], in0=ii[:, t0:t1], in1=xc[:, t0:t1], op=ALU.mult), r=[K("ii"), K("xc")], w=[K("ii")])
            yield
            if passB and hf == 0:
                P.add("dve", lambda e: e.tensor_scalar(out=ii[:, 0:128], in0=ii[:, 0:128], scalar1=flag[:, 0:1], scalar2=None, op0=ALU.mult),
                      r=[K("ii"), "flag"], w=[K("ii")])
            if hf == 0:
                if passB:
                    P.add("dve", lambda e: e.tensor_tensor_scan(out=rr[:, t0:t1], data0=aa[:, t0:t1], data1=ii[:, t0:t1], initial=carry[:, c:c + 1],
                                                                op0=ALU.mult, op1=ALU.add), r=[K("aa"), K("ii"), "carry"], w=[K("rr")])
                else:
                    P.add("dve", lambda e: e.tensor_tensor_scan(out=rr[:, t0:t1], data0=aa[:, t0:t1], data1=ii[:, t0:t1], initial=0.0,
                                                                op0=ALU.mult, op1=ALU.add), r=[K("aa"), K("ii")], w=[K("rr")])
            else:
                P.add("dve", lambda e: e.tensor_tensor_scan(out=rr[:, t0:t1], data0=aa[:, t0:t1], data1=ii[:, t0:t1], initial=hcar[:, hf - 1:hf],
                                                            op0=ALU.mult, op1=ALU.add), r=[K("aa"), K("ii"), f"hcar{hf - 1}"], w=[K("rr")])
            if hf < NP - 1:
                P.add("dve", lambda e: e.tensor_copy(out=hcar[:, hf:hf + 1], in_=rr[:, t1 - 1:t1]), r=[K("rr")], w=[f"hcar{hf}"])
            elif not passB:
                P.add("dve", lambda e: e.tensor_scalar(out=carry[:, c:c + 1], in0=rr[:, t1 - 1:t1], scalar1=flag[:, 0:1], scalar2=None, op0=ALU.mult),
                      r=[K("rr"), "flag"], w=["carry"])
                P.add("dve", lambda e: e.tensor_copy(out=rawhalo[:, c, :], in_=raw[:, t1:t1 + 3]), r=[K("raw")], w=["rawhalo"])
            yield
            if not passB:
                return
            for (c0, c1) in blks:
                pt, pk = psum()
                pt_key[0] = pk
                fm_matmul(wt2, wk2, c0, c1, pt)
                P.add("act", lambda e, pt=pt, c0=c0, c1=c1: e.activation(out=xc[:, c0:c1], in_=pt[:, 0:c1 - c0], func=AF.Copy), r=[pk], w=[K("xc")])
                yield
            P.add("act", lambda e: e.activation(out=ii[:, t0:t1], in_=xc[:, t0:t1], func=AF.Square), r=[K("xc")], w=[K("ii")])
            yield
            P.add("pool", lambda e: e.tensor_scalar(out=ii[:, t0:t1], in0=ii[:, t0:t1], scalar1=0.044715, scalar2=1.0, op0=ALU.mult, op1=ALU.add), r=[K("ii")], w=[K("ii")])
            yield
            P.add("dve", lambda e: e.tensor_tensor(out=ii[:, t0:t1], in0=ii[:, t0:t1], in1=xc[:, t0:t1], op=ALU.mult), r=[K("ii"), K("xc")], w=[K("ii")])
            yield
            P.add("act", lambda e: e.activation(out=ii[:, t0:t1], in_=ii[:, t0:t1], func=AF.Sigmoid, scale=1.5957691216057308), r=[K("ii")], w=[K("ii")])
            yield
            P.add("pool", lambda e: e.tensor_tensor(out=ii[:, t0:t1], in0=ii[:, t0:t1], in1=xc[:, t0:t1], op=ALU.mult), r=[K("ii"), K("xc")], w=[K("ii")])
            yield
            P.add("dve", lambda e: e.tensor_tensor(out=rr[:, t0:t1], in0=rr[:, t0:t1], in1=ii[:, t0:t1], op=ALU.mult), r=[K("ii"), K("rr")], w=[K("rr")])
            yield
            for (c0, c1) in blks:
                pt, pk = psum()
                pt_key[0] = pk
                fm_matmul(wt3, wk3, c0, c1, pt)
                P.add("act", lambda e, pt=pt, c0=c0, c1=c1: e.activation(out=aa[:, c0:c1], in_=pt[:, 0:c1 - c0], func=AF.Sigmoid), r=[pk], w=[K("aa")])
                yield
            P.add("dve", lambda e: e.tensor_tensor(out=rr[:, t0:t1], in0=rr[:, t0:t1], in1=aa[:, t0:t1], op=ALU.mult), r=[K("aa"), K("rr")], w=[K("rr")])
            P.add("sp", lambda e: e.dma_start(out=AT_d[c * 128:(c + 1) * 128, t0:t1], in_=rr[:, t0:t1]), r=[K("rr")], dsem=K("st_rr"))
            yield
            for (c0, c1) in blks:
                pt, pk = psum()
                pt_key[0] = pk
                fm_matmul(wt4, wk4, c0, c1, pt)
                P.add("act", lambda e, pt=pt, c0=c0, c1=c1: e.activation(out=aa[:, c0:c1], in_=pt[:, 0:c1 - c0], func=AF.Sigmoid), r=[pk], w=[K("aa")])
                yield
            P.add("sp", lambda e: e.dma_start(out=GT_d[c * 128:(c + 1) * 128, t0:t1], in_=aa[:, t0:t1]), r=[K("aa")], dsem=K("st_aa"))
            yield

        def rg_stream():
            nxt = None
            th = Tn // NP
            for c in range(16):
                if nxt is None:
                    wt, wk = wfm.next()
                    load_piece(winb[c], wt[:], [wk])
                else:
                    wt, wk = nxt
                wt2 = wk2 = wt3 = wk3 = wt4 = wk4 = None
                if passB:
                    wt2, wk2 = wfm.next()
                    load_piece(winb[16 + c], wt2[:], [wk2])
                    wt3, wk3 = wfm.next()
                    load_piece(winb[65 + c], wt3[:], [wk3])
                    wt4, wk4 = wfm.next()
                    load_piece(winb[81 + c], wt4[:], [wk4])
                if c < 15:
                    nxt = wfm.next()
                    load_piece(winb[c + 1], nxt[0][:], [nxt[1]])
                wts = (wt, wk, wt2, wk2, wt3, wk3, wt4, wk4)
                bnd = [0, 512, 1024, 1536, Tn]
                live = [rg_half(c, k_, bnd[k_], bnd[k_ + 1], wts) for k_ in range(NP)]
                while live:
                    for g_ in list(live):
                        try:
                            next(g_)
                        except StopIteration:
                            live.remove(g_)
                    yield

        ntile = NQT if passB else 16
        wtile0 = 15 if passB else 0

        pend_fin = [None]

        def tm_group(b0, nblk, epilogue):
            wt, wk = wtm.next()
            for j in range(nblk):
                load_piece(winb[b0 + j], wt[:, :, j * 128:(j + 1) * 128], [wk])
            for ti in range(ntile):
                pt, pk = psum()
                for kc in range(16):
                    P.add("pe", lambda e, kc=kc, pt=pt, ti=ti: e.matmul(pt[:, 0:nblk * 128], lhsT=xtb[:, kc, ti * 128:(ti + 1) * 128],
                                                                        rhs=wt[:, kc, 0:nblk * 128], start=(kc == 0), stop=(kc == 15)),
                          r=[wk, "xtb"], w=[pk])
                if pend_fin[0] is not None:
                    pend_fin[0]()
                pend_fin[0] = epilogue(ti, pt, pk)
                yield
            if pend_fin[0] is not None:
                pend_fin[0]()
                pend_fin[0] = None

        def ep_k(ti, pt, pk):
            ob, ok = tmo.next()
            rope_epilogue(pt, pk, wtile0 + ti, 4, 128, 16, tabq, ob, ok)
            wt_ = wtile0 + ti
            return lambda: transpose_store(ob, ok, 4, lambda j: KT_d[j, :, wt_ * 128:(wt_ + 1) * 128])

        def ep_v(ti, pt, pk):
            ob, ok = tmo.next()
            P.add("act", lambda e: e.activation(out=ob[:], in_=pt[:], func=AF.Copy), r=[pk], w=[ok])
            wt_ = wtile0 + ti
            P.add("sp", lambda e: e.dma_start(out=V_d[wt_ * 128:(wt_ + 1) * 128, :], in_=ob[:]), r=[ok], dsem=ok + "_st")
            return None

        def ep_kiwi(ti, pt, pk):
            ob, ok = tmo.next()
            rope_epilogue(pt, pk, wtile0 + ti, 1, 64, 8, tabi, ob, ok)
            P.add("pool", lambda e: e.tensor_copy(out=ob[:, 64:128], in_=ob[:, 0:64]), r=[ok], w=[ok])
            wt_ = wtile0 + ti
            if passB:
                P.add("dve", lambda e: e.tensor_scalar(out=wis[:, ti, :], in0=pt[:, 64:80], scalar1=0.25 * 0.125, scalar2=None, op0=ALU.mult),
                      r=[pk], w=["wis"])
            return lambda: transpose_store(ob, ok, 1, lambda j: KIT_d[:, wt_ * 128:(wt_ + 1) * 128])

        def ep_q(g):
            def f(ti, pt, pk):
                ob, ok = tmo.next()
                rope_epilogue(pt, pk, wtile0 + ti, 4, 128, 16, tabq, ob, ok)
                return lambda: transpose_store(ob, ok, 4, lambda j: QT_d[g * 4 + j, :, ti * 128:(ti + 1) * 128])
            return f

        def ep_qi(g):
            def f(ti, pt, pk):
                ob, ok = tmo.next()
                rope_epilogue(pt, pk, wtile0 + ti, 8, 64, 8, tabi, ob, ok)
                return lambda: transpose_store(ob, ok, 4, lambda j: QIT_d[g * 4 + j, :, ti * 128:(ti + 1) * 128])
            return f

        def tm_stream():
            yield from tm_group(48, 4, ep_k)
            yield from tm_group(52, 4, ep_v)
            yield from tm_group(64, 1, ep_kiwi)
            if passB:
                for g in range(4):
                    yield from tm_group(32 + 4 * g, 4, ep_q(g))
                for g in range(2):
                    yield from tm_group(56 + 4 * g, 4, ep_qi(g))

        run_streams([(rg_stream(), 2 if passB else 3), (tm_stream(), 1)])

    rglru_pass(False)
    rglru_pass(True)
    P.barrier()

    P.off = base_off
    KT = P.sb([128, 4, T], BF16, "KT")
    Vs = P.sb([128, 32, 512], BF16, "Vs")
    kiT = P.sb([128, T], BF16, "kiT")
    kbias = P.sb([128, 1], F32, "kbias")
    tri = load_small(tri_d, [128, 128], "tri")
    acc2 = [P.sb([128, T], F32, "acc0"), P.sb([128, T], F32, "acc1")]
    maskb = P.sb([128, T], BF16, "maskb")
    maskT2 = [P.sb([128, 32, 128], BF16, "maskT0"), P.sb([128, 32, 128], BF16, "maskT1")]
    QTr = Ring(P, "QTr", 1, [128, 16, 128], BF16)
    QIr = Ring(P, "QIr", 2, [128, 8, 128], BF16)
    relr = Ring(P, "relr", 4, [128, 512], BF16)
    dgr = Ring(P, "dgr", 2, [128, 16, 128], BF16)
    PTr = Ring(P, "PTr", 4, [128, 512], BF16)
    ATr = Ring(P, "ATr", 1, [128, 16, 128], F32)
    GTr = Ring(P, "GTr", 1, [128, 16, 128], F32)
    MTr = Ring(P, "MTr", 1, [128, 16, 128], BF16)
    rdr = Ring(P, "rdr", 1, [128, 512], F32)
    yr = Ring(P, "yr", 1, [128, 512], F32)
    bis = P.sb([128, 4], F32, "bis")

    for n in range(4):
        P.add("sp", lambda e, n=n: e.dma_start(out=KT[:, n, :], in_=KT_d[n]), w=["KT"], dsem="ld_KT")
    for q in range(4):
        P.add("sp", lambda e, q=q: e.dma_start(out=Vs[:, q * 8:(q + 1) * 8, :],
                                               in_=V_d[q * 1024:(q + 1) * 1024, :].rearrange("(c p) d -> p c d", p=128)), w=["Vs"], dsem="ld_V")
    P.add("sp", lambda e: e.dma_start(out=kiT[:], in_=KIT_d), w=["kiT"], dsem="ld_ki")
    P.add("sp", lambda e: e.dma_start(out=kbias[:], in_=kbias_d), w=["kbias"], dsem="ld_kb")

    prange[0] = 6
    NIT = 17
    B0 = 8.0
    SCALE = 128.0 ** -0.5

    def stageA(qt):
        W = (15 + qt + 1) * 128
        acc = acc2[qt % 2]
        ack = f"acc{qt % 2}"
        qiT, qik = QIr.next()
        P.add("sp", lambda e: e.dma_start(out=qiT[:], in_=QIT_d[:, :, qt * 128:(qt + 1) * 128].rearrange("h p t -> p h t")), w=[qik], dsem=qik)
        dg, dgk = dgr.next()
        for h in range(16):
            P.add("dve", lambda e, h=h: e.tensor_scalar(out=dg[:, h, :], in0=ident[:], scalar1=wis[:, qt, h:h + 1], scalar2=None, op0=ALU.mult),
                  r=["ident", "wis"], w=[dgk])
        yield
        pa, pak = ps[6], "ps6"
        for (c0, c1) in blocks(W):
            nn = c1 - c0
            pend = []

            def emit_acc(item, c0=c0, c1=c1, nn=nn):
                h_, rl_, rk_ = item
                P.add("pe", lambda e, h_=h_, rl_=rl_, nn=nn: e.matmul(pa[:, 0:nn], lhsT=dg[:, h_, :], rhs=rl_[:, 0:nn], start=(h_ == 0), stop=(h_ == 15)),
                      r=[dgk, rk_], w=[pak])

            for h in range(16):
                pt, pk = psum_fixed("pA", 4, 2)
                pb = (h % 2) * 64
                P.add("pe", lambda e, pt=pt, h=h, pb=pb, c0=c0, c1=c1: e.matmul(pt[:, 0:c1 - c0], lhsT=qiT[pb:pb + 64, h // 2, :],
                                                                                rhs=kiT[pb:pb + 64, c0:c1], start=True, stop=True),
                      r=[qik, "kiT"], w=[pk])
                rl, rk = relr.next()
                P.add("act", lambda e, pt=pt, rl=rl, nn=nn: e.activation(out=rl[:, 0:nn], in_=pt[:, 0:nn], func=AF.Relu), r=[pk], w=[rk])
                pend.append((h, rl, rk))
                if len(pend) > 1:
                    emit_acc(pend.pop(0))
                yield
            while pend:
                emit_acc(pend.pop(0))
            if c1 <= 2048:
                P.add("dve", lambda e, c0=c0, c1=c1, nn=nn: e.tensor_scalar(out=acc[:, c0:c1], in0=pa[:, 0:nn], scalar1=kbias[:, 0:1], scalar2=None, op0=ALU.add),
                      r=[pak, "kbias"], w=[ack])
            else:
                P.add("dve", lambda e, c0=c0, c1=c1, nn=nn: e.tensor_copy(out=acc[:, c0:c1], in_=pa[:, 0:nn]), r=[pak], w=[ack])
            yield
        P.add("dve", lambda e: e.tensor_tensor(out=acc[:, W - 128:W], in0=acc[:, W - 128:W], in1=tri[:], op=ALU.add), r=[ack, "tri"], w=[ack])
        yield

    def stageB(qt):
        W = (15 + qt + 1) * 128
        nch = W // 128
        acc = acc2[qt % 2]
        ack = f"acc{qt % 2}"
        maskT = maskT2[qt % 2]
        mtk_ = f"maskT{qt % 2}"
        P.add("dve", lambda e: e.memset(bis[:, 0:1], 0.0), w=["bis"])
        for it in range(NIT):
            wk_ = B0 / (2.0 ** it)
            P.add("dve", lambda e: e.tensor_scalar(out=maskb[:, 0:W], in0=acc[:, 0:W], scalar1=bis[:, 0:1], scalar2=0.0, op0=ALU.is_ge, op1=ALU.add,
                                                   accum_out=bis[:, 1:2]), r=[ack, "bis"], w=["maskb", "bis"])
            P.add("dve", lambda e, wk_=wk_: e.tensor_scalar(out=bis[:, 2:3], in0=bis[:, 1:2], scalar1=255.5, scalar2=wk_, op0=ALU.is_ge, op1=ALU.mult),
                  r=["bis"], w=["bis"])
            if it < NIT - 1:
                P.add("dve", lambda e, wk_=wk_: e.scalar_tensor_tensor(out=bis[:, 0:1], in0=bis[:, 2:3], scalar=-wk_ / 2.0, in1=bis[:, 0:1], op0=ALU.add, op1=ALU.add),
                      r=["bis"], w=["bis"])
            else:
                P.add("dve", lambda e, wk_=wk_: e.scalar_tensor_tensor(out=bis[:, 3:4], in0=bis[:, 2:3], scalar=-wk_, in1=bis[:, 0:1], op0=ALU.add, op1=ALU.add),
                      r=["bis"], w=["bis"])
            yield
        P.add("dve", lambda e: e.tensor_scalar(out=maskb[:, 0:W], in0=acc[:, 0:W], scalar1=bis[:, 3:4], scalar2=None, op0=ALU.is_ge), r=[ack, "bis"], w=["maskb"])
        yield
        for g0 in range(0, nch, 8):
            g1 = min(g0 + 8, nch)
            pt, pk = ps[7], "ps7"
            ptb = pt[:].bitcast(BF16)
            for sc in range(g0, g1):
                P.add("pe", lambda e, sc=sc, g0=g0, ptb=ptb: e.transpose(out=ptb[:, (sc - g0) * 128:(sc - g0 + 1) * 128], in_=maskb[:, sc * 128:(sc + 1) * 128],
                                                                         identity=ident[:]), r=["maskb", "ident"], w=[pk])
            P.add("act", lambda e, g0=g0, g1=g1, ptb=ptb: e.activation(out=maskT[:, g0:g1, :].rearrange("p c q -> p (c q)"), in_=ptb[:, 0:(g1 - g0) * 128], func=AF.Copy),
                  r=[pk], w=[mtk_])
            yield

    def stageD(qt):
        W = (15 + qt + 1) * 128
        nch = W // 128
        maskT = maskT2[qt % 2]
        mtk_ = f"maskT{qt % 2}"
        qT, qk = QTr.next()
        P.add("sp", lambda e: e.dma_start(out=qT[:], in_=QT_d[:, :, qt * 128:(qt + 1) * 128].rearrange("h p t -> p h t")), w=[qk], dsem=qk)
        aT, ak = ATr.next()
        gT, gk = GTr.next()
        P.add("sp", lambda e: e.dma_start(out=aT[:], in_=AT_d[:, qt * 128:(qt + 1) * 128].rearrange("(c p) t -> p c t", p=128)), w=[ak], dsem=ak)
        P.add("sp", lambda e: e.dma_start(out=gT[:], in_=GT_d[:, qt * 128:(qt + 1) * 128].rearrange("(c p) t -> p c t", p=128)), w=[gk], dsem=gk)
        mT, mk = MTr.next()
        for n in range(4):
            po, pok = psum_fixed("po", 0, 1)
            pd, pdk = psum_fixed("pd", 1, 1)
            pend = []

            def emit_pv(item, n=n, po=po, pd=pd, pok=pok, pdk=pdk):
                sc_, pT_, ptk_ = item
                P.add("pe", lambda e, sc_=sc_, pT_=pT_, n=n, po=po: e.matmul(po[:], lhsT=Vs[:, sc_, n * 128:(n + 1) * 128], rhs=pT_[:], start=(sc_ == 0), stop=(sc_ == nch - 1)),
                      r=["Vs", ptk_], w=[pok])
                P.add("pe", lambda e, sc_=sc_, pT_=pT_, pd=pd: e.matmul(pd[:], lhsT=ones[:], rhs=pT_[:], start=(sc_ == 0), stop=(sc_ == nch - 1)),
                      r=["ones", ptk_], w=[pdk])

            for sc in range(nch):
                pt, pk = psum_fixed("pS", 2, 2)
                P.add("pe", lambda e, pt=pt, sc=sc, n=n: e.matmul(pt[:], lhsT=KT[:, n, sc * 128:(sc + 1) * 128], rhs=qT[:, 4 * n:4 * n + 4, :],
                                                             start=True, stop=True), r=["KT", qk], w=[pk])
                pT, ptk = PTr.next()
                P.add("act", lambda e, pt=pt, pT=pT: e.activation(out=pT[:], in_=pt[:], func=AF.Exp, scale=SCALE), r=[pk], w=[ptk])
                P.add("dve" if sc % 3 == 2 else "pool", lambda e, pT=pT, sc=sc: e.tensor_tensor(out=pT[:].rearrange("p (g q) -> p g q", g=4), in0=pT[:].rearrange("p (g q) -> p g q", g=4),
                                                                      in1=maskT[:, sc, :].unsqueeze(1).to_broadcast([128, 4, 128]), op=ALU.mult),
                      r=[ptk, mtk_], w=[ptk])
                pend.append((sc, pT, ptk))
                if len(pend) > 2:
                    emit_pv(pend.pop(0))
                yield
            while pend:
                emit_pv(pend.pop(0))
            rd, rdk = rdr.next()
            P.add("dve", lambda e, rd=rd, pd=pd: e.tensor_scalar(out=rd[:], in0=pd[:], scalar1=1e-30, scalar2=None, op0=ALU.max), r=[pdk], w=[rdk])
            P.add("dve", lambda e, rd=rd: e.reciprocal(out=rd[:], in_=rd[:]), r=[rdk], w=[rdk])
            yy, yk = yr.next()
            P.add("dve", lambda e, yy=yy, po=po, rd=rd: e.tensor_tensor(out=yy[:], in0=po[:], in1=rd[:], op=ALU.mult), r=[pok, rdk], w=[yk])
            P.add("pool", lambda e, yy=yy, n=n: e.tensor_tensor(out=yy[:], in0=yy[:], in1=gT[:, 4 * n:4 * n + 4, :].rearrange("p c t -> p (c t)"), op=ALU.mult),
                  r=[yk, gk], w=[yk])
            P.add("pool", lambda e, yy=yy, n=n: e.tensor_tensor(out=mT[:, 4 * n:4 * n + 4, :].rearrange("p c t -> p (c t)"), in0=yy[:],
                                                                in1=aT[:, 4 * n:4 * n + 4, :].rearrange("p c t -> p (c t)"), op=ALU.add),
                  r=[yk, ak], w=[mk])
            yield
        P.add("sp", lambda e: e.dma_start(out=MT_d[:, qt * 128:(qt + 1) * 128].rearrange("(c p) t -> p c t", p=128), in_=mT[:]), r=[mk], dsem=mk + "_st")
        yield

    for m in range(NQT + 2):
        streams = []
        if m < NQT:
            streams.append((stageA(m), 5))
        if 0 <= m - 1 < NQT:
            streams.append((stageB(m - 1), 1))
        if 0 <= m - 2 < NQT:
            streams.append((stageD(m - 2), 5))
        run_streams(streams)
    P.barrier()

    P.off = base_off
    prange[0] = 0
    wob = P.sb([128, 16, D], BF16, "wob")
    lng = P.sb([128, D], F32, "lng")
    lnb = P.sb([128, D], F32, "lnb")
    mtr = Ring(P, "mtr", 2, [128, 16, 128], BF16)
    xr_ = Ring(P, "xres", 2, [128, D], F32)
    zr = Ring(P, "zr", 2, [128, D], F32)
    zb = Ring(P, "zb", 2, [128, D], BF16)
    x1t = Ring(P, "x1t", 2, [128, 16, 128], BF16)
    lnst = Ring(P, "lnst", 2, [128, 8], F32)
    sqj = P.sb([128, D], F32, "sqj")

    def layer_norm(z, zk, g, gkey, b, bkey):
        st, sk = lnst.next()
        P.add("dve", lambda e: e.reduce_sum(out=st[:, 0:1], in_=z[:], axis=mybir.AxisListType.X), r=[zk], w=[sk])
        yield
        P.add("dve", lambda e: e.tensor_scalar(out=st[:, 1:2], in0=st[:, 0:1], scalar1=-1.0 / D, scalar2=None, op0=ALU.mult), r=[sk], w=[sk])
        P.add("act", lambda e: e.activation(out=sqj[:], in_=z[:], func=AF.Square, bias=st[:, 1:2], accum_out=st[:, 2:3]), r=[zk, sk], w=["sqj", sk])
        yield
        P.add("dve", lambda e: e.tensor_scalar(out=st[:, 3:4], in0=st[:, 2:3], scalar1=1.0 / D, scalar2=1e-5, op0=ALU.mult, op1=ALU.add), r=[sk], w=[sk])
        P.add("act", lambda e: e.activation(out=st[:, 3:4], in_=st[:, 3:4], func=AF.Ln), r=[sk], w=[sk])
        P.add("act", lambda e: e.activation(out=st[:, 3:4], in_=st[:, 3:4], func=AF.Exp, scale=-0.5), r=[sk], w=[sk])
        yield
        P.add("dve", lambda e: e.tensor_scalar(out=z[:], in0=z[:], scalar1=st[:, 1:2], scalar2=st[:, 3:4], op0=ALU.add, op1=ALU.mult), r=[zk, sk], w=[zk])
        yield
        P.add("dve", lambda e: e.tensor_tensor(out=z[:], in0=z[:], in1=g[:], op=ALU.mult), r=[zk, gkey], w=[zk])
        yield
        P.add("dve", lambda e: e.tensor_tensor(out=z[:], in0=z[:], in1=b[:], op=ALU.add), r=[zk, bkey], w=[zk])


    def run_pool(gens, nactive):
        gens = list(gens)
        active = []
        while gens or active:
            while len(active) < nactive and gens:
                active.append(gens.pop(0))
            for g_ in list(active):
                try:
                    next(g_)
                except StopIteration:
                    active.remove(g_)

    for kc in range(16):
        load_piece(wout[kc * 128:(kc + 1) * 128, :], wob[:, kc, :], ["wob"])
    P.add("sp", lambda e: e.dma_start(out=lng[:], in_=ln1g_d), w=["lng"], dsem="ld_lng")
    P.add("sp", lambda e: e.dma_start(out=lnb[:], in_=ln1b_d), w=["lnb"], dsem="ld_lnb")
    def p3_tile(qt):
        if qt == 1:
            for _ in range(6):
                yield
        mt, mtk = mtr.next()
        P.add("sp", lambda e, mt=mt, qt=qt: e.dma_start(out=mt[:], in_=MT_d[:, qt * 128:(qt + 1) * 128].rearrange("(c p) t -> p c t", p=128)), w=[mtk], dsem=mtk)
        xt_, xk = xr_.next()
        if qt == 0:
            P.add("sp", lambda e, xt_=xt_: e.dma_start(out=xt_[:], in_=xhalo), w=[xk], dsem=xk)
        else:
            P.add("sp", lambda e, xt_=xt_, qt=qt: e.dma_start(out=xt_[:], in_=xown[(qt - 1) * 128:qt * 128, :]), w=[xk], dsem=xk)
        z, zk = zr.next()
        for db in range(4):
            pt, pk = psum()
            for kc in range(16):
                P.add("pe", lambda e, pt=pt, kc=kc, mt=mt, db=db: e.matmul(pt[:], lhsT=mt[:, kc, :], rhs=wob[:, kc, db * 512:(db + 1) * 512],
                                                                           start=(kc == 0), stop=(kc == 15)), r=[mtk, "wob"], w=[pk])
            P.add("dve", lambda e, pt=pt, z=z, xt_=xt_, db=db: e.scalar_tensor_tensor(out=z[:, db * 512:(db + 1) * 512], in0=xt_[:, db * 512:(db + 1) * 512], scalar=ALPHA,
                                                                                      in1=pt[:], op0=ALU.mult, op1=ALU.add), r=[pk, xk], w=[zk])
            yield
        yield from layer_norm(z, zk, lng, "lng", lnb, "lnb")
        P.add("sp", lambda e, z=z, qt=qt: e.dma_start(out=X1_d[qt * 128:(qt + 1) * 128, :], in_=z[:]), r=[zk], dsem=zk + "_st")
        zbt, zbk = zb.next()
        P.add("act", lambda e, zbt=zbt, z=z: e.activation(out=zbt[:], in_=z[:], func=AF.Copy), r=[zk], w=[zbk])
        yield
        xo, xok = x1t.next()
        for h2 in range(2):
            pt, pk = psum()
            ptb = pt[:].bitcast(BF16)
            for j in range(8):
                cc = h2 * 8 + j
                P.add("pe", lambda e, ptb=ptb, j=j, cc=cc, zbt=zbt: e.transpose(out=ptb[:, j * 128:(j + 1) * 128], in_=zbt[:, cc * 128:(cc + 1) * 128], identity=ident[:]),
                      r=[zbk, "ident"], w=[pk])
            P.add("act", lambda e, ptb=ptb, xo=xo, h2=h2: e.activation(out=xo[:, h2 * 8:(h2 + 1) * 8, :].rearrange("p c t -> p (c t)"), in_=ptb[:, 0:1024], func=AF.Copy),
                  r=[pk], w=[xok])
            yield
        P.add("sp", lambda e, xo=xo, qt=qt: e.dma_start(out=X1T_d[:, qt * 128:(qt + 1) * 128].rearrange("(c p) t -> p c t", p=128), in_=xo[:]), r=[xok], dsem=xok + "_st")
    run_pool([p3_tile(qt) for qt in range(NQT)], 2)
    P.barrier()

    P.off = base_off
    NS = 1024
    x1T = P.sb([128, 16, NS + 2], BF16, "x1T")
    actT = P.sb([128, 44, NS], BF16, "actT")
    rawg = Ring(P, "rawg", 2, [128, NS + 2], F32)
    cvg = Ring(P, "cvg", 1, [128, NS], F32)
    cvu = Ring(P, "cvu", 1, [128, NS], F32)
    wup = Ring(P, "wup", 4, [128, 16, 128], BF16)
    wdn = Ring(P, "wdn", 4, [128, 4, 512], BF16)
    x1r = Ring(P, "x1r", 2, [128, 512], F32)
    yo = Ring(P, "yo", 2, [128, 512], F32)
    for sb_ in range(2):
        t0 = 128 + sb_ * NS
        P.add("sp", lambda e, t0=t0: e.dma_start(out=x1T[:], in_=X1T_d[:, t0 - 2:t0 + NS].rearrange("(c p) t -> p c t", p=128)), w=["x1T"], dsem="ld_x1T")
        order = [half_ * 44 + i for i in range(44) for half_ in range(2)]
        wq = []

        def up_prefetch():
            if order:
                fi_ = order.pop(0)
                wt_, wk_ = wup.next()
                load_piece(wupb[fi_], wt_[:], [wk_])
                wq.append((wt_, wk_))

        up_prefetch(); up_prefetch()
        for i in range(44):
            cvs = []
            for half_ in range(2):
                fi = half_ * 44 + i
                up_prefetch()
                wt, wk = wq.pop(0)
                rw, rwk = rawg.next()
                for (c0, c1) in [(0, 2), (2, 514), (514, 1026)]:
                    pt, pk = psum()
                    for kc in range(16):
                        P.add("pe", lambda e, pt=pt, wt=wt, kc=kc, c0=c0, c1=c1: e.matmul(pt[:, 0:c1 - c0], lhsT=wt[:, kc, :], rhs=x1T[:, kc, c0:c1],
                                                                                           start=(kc == 0), stop=(kc == 15)), r=[wk, "x1T"], w=[pk])
                    if c0 == 0 and sb_ == 0:
                        P.add("dve", lambda e, pt=pt, rw=rw: e.tensor_scalar(out=rw[:, 0:2], in0=pt[:, 0:2], scalar1=flag[:, 0:1], scalar2=None, op0=ALU.mult),
                              r=[pk, "flag"], w=[rwk])
                    else:
                        P.add("act", lambda e, pt=pt, rw=rw, c0=c0, c1=c1: e.activation(out=rw[:, c0:c1], in_=pt[:, 0:c1 - c0], func=AF.Copy), r=[pk], w=[rwk])
                cv, cvk = (cvg if half_ == 0 else cvu).next()
                P.add("dve", lambda e, cv=cv, rw=rw, fi=fi: e.tensor_scalar(out=cv[:], in0=rw[:, 2:2 + NS], scalar1=fcw[:, fi, 2:3], scalar2=fcb[:, fi:fi + 1],
                                                                            op0=ALU.mult, op1=ALU.add), r=[rwk, "fcw", "fcb"], w=[cvk])
                for j in (1, 0):
                    P.add("dve", lambda e, cv=cv, rw=rw, fi=fi, j=j: e.scalar_tensor_tensor(out=cv[:], in0=rw[:, j:j + NS], scalar=fcw[:, fi, j:j + 1], in1=cv[:],
                                                                                            op0=ALU.mult, op1=ALU.add), r=[rwk, "fcw", cvk], w=[cvk])
                cvs.append((cv, cvk))
            (cg, cgk), (cu, cuk) = cvs
            P.add("act", lambda e, cg=cg: e.activation(out=cg[:], in_=cg[:], func=AF.Silu), r=[cgk], w=[cgk])
            P.add("pool", lambda e, cg=cg, cu=cu, i=i: e.tensor_tensor(out=actT[:, i, :], in0=cg[:], in1=cu[:], op=ALU.mult), r=[cgk, cuk], w=["actT"])
        dorder = [(db, q4) for db in range(4) for q4 in range(11)]
        dq = []

        def dn_prefetch():
            if dorder:
                db_, q4_ = dorder.pop(0)
                wd_, wdk_ = wdn.next()
                load_piece(wdown[q4_ * 512:(q4_ + 1) * 512, db_ * 512:(db_ + 1) * 512].rearrange("(f p) d -> p f d", p=128), wd_[:], [wdk_])
                dq.append((wd_, wdk_))

        dn_prefetch(); dn_prefetch()
        for db in range(4):
            pts = [(ps[k], f"ps{k}") for k in range(8)]
            for q4 in range(11):
                dn_prefetch()
                wd, wdk = dq.pop(0)
                for tt_ in range(8):
                    pt, pk = pts[tt_]
                    for f4 in range(4):
                        fc = q4 * 4 + f4
                        P.add("pe", lambda e, pt=pt, wd=wd, f4=f4, fc=fc, tt_=tt_: e.matmul(pt[:], lhsT=actT[:, fc, tt_ * 128:(tt_ + 1) * 128], rhs=wd[:, f4, :],
                                                                                            start=(fc == 0), stop=(fc == 43)), r=["actT", wdk], w=[pk])
            for tt_ in range(8):
                pt, pk = pts[tt_]
                row0 = t0 + tt_ * 128
                xx, xxk = x1r.next()
                P.add("sp", lambda e, xx=xx, row0=row0, db=db: e.dma_start(out=xx[:], in_=X1_d[row0:row0 + 128, db * 512:(db + 1) * 512]), w=[xxk], dsem=xxk)
                y_, yk = yo.next()
                P.add("dve", lambda e, y_=y_, xx=xx, pt=pt: e.scalar_tensor_tensor(out=y_[:], in0=xx[:], scalar=ALPHA, in1=pt[:], op0=ALU.mult, op1=ALU.add),
                      r=[pk, xxk], w=[yk])
                P.add("sp", lambda e, y_=y_, row0=row0, db=db: e.dma_start(out=Y_d[row0 - 128:row0, db * 512:(db + 1) * 512], in_=y_[:]), r=[yk], dsem=yk + "_st")
    P.barrier()

    P.add("sp", lambda e: e.dma_start(out=lng[:], in_=ln2g_d), w=["lng"], dsem="ld_lng")
    P.add("sp", lambda e: e.dma_start(out=lnb[:], in_=ln2b_d), w=["lnb"], dsem="ld_lnb")
    outkeys = set()
    def p5_tile(tt_):
        if tt_ == 1:
            for _ in range(4):
                yield
        z, zk = zr.next()
        P.add("sp", lambda e: e.dma_start(out=z[:], in_=Y_d[tt_ * 128:(tt_ + 1) * 128, :]), w=[zk], dsem=zk + "_ld")
        yield
        yield from layer_norm(z, zk, lng, "lng", lnb, "lnb")
        P.add("sp", lambda e: e.dma_start(out=out_d[tt_ * 128:(tt_ + 1) * 128, :], in_=z[:]), r=[zk], dsem=zk + "_st")
        outkeys.add(zk + "_st")
        yield

    run_pool([p5_tile(t_) for t_ in range(16)], 2)
    P.emit(final_waits=sorted(outkeys))
    return nc, P


def _blk(w):
    K, M = w.shape
    return np.ascontiguousarray(w.reshape(K // 128, 128, M // 128, 128).transpose(2, 1, 0, 3))


def _fm(v):
    return np.ascontiguousarray(v.reshape(-1, 128).T)


_NC_CACHE = {}


def kernel(x, w_in, rnn_conv_w, rnn_conv_b, lru_wa, lru_ba, lru_wi, lru_bi, lru_lambda, w_out, ln1_g, ln1_b,
           w_up, ffn_conv_w, ffn_conv_b, w_down, ln2_g, ln2_b, _debug=False):
    f32 = np.float32
    x = np.asarray(x, f32)
    wi_ = np.asarray(w_in, f32)[0]
    pad = np.zeros((D, 48), f32)
    w_in_p = np.concatenate([wi_[:, 0:8272], pad, wi_[:, 8272:]], axis=1)
    shared = {
        "winb": _blk(w_in_p),
        "wout": np.ascontiguousarray(np.asarray(w_out, f32)[0]),
        "wupb": _blk(np.asarray(w_up, f32)[0]),
        "wdown": np.ascontiguousarray(np.asarray(w_down, f32)[0]),
        "cw": np.ascontiguousarray(np.asarray(rnn_conv_w, f32)[0].reshape(4, 16, 128).transpose(2, 1, 0)),
        "cb": _fm(np.asarray(rnn_conv_b, f32)[0]),
        "lba": _fm(np.asarray(lru_ba, f32)[0].reshape(-1)),
        "lbi": _fm(np.asarray(lru_bi, f32)[0].reshape(-1)),
        "lam": _fm(np.asarray(lru_lambda, f32)[0]),
        "lwa": np.ascontiguousarray(np.asarray(lru_wa, f32)[0].transpose(1, 0, 2).reshape(128, 2048)),
        "lwi": np.ascontiguousarray(np.asarray(lru_wi, f32)[0].transpose(1, 0, 2).reshape(128, 2048)),
        "fcw": np.ascontiguousarray(np.asarray(ffn_conv_w, f32)[0].reshape(3, 88, 128).transpose(2, 1, 0)),
        "fcb": _fm(np.asarray(ffn_conv_b, f32)[0]),
        "ln1g": np.ascontiguousarray(np.broadcast_to(np.asarray(ln1_g, f32)[0], (128, D))),
        "ln1b": np.ascontiguousarray(np.broadcast_to(np.asarray(ln1_b, f32)[0], (128, D))),
        "ln2g": np.ascontiguousarray(np.broadcast_to(np.asarray(ln2_g, f32)[0], (128, D))),
        "ln2b": np.ascontiguousarray(np.broadcast_to(np.asarray(ln2_b, f32)[0], (128, D))),
        "tri": np.where(np.arange(128)[None, :] <= np.arange(128)[:, None], 0.0, NEG).astype(f32),
    }
    in_maps = []
    for core in range(8):
        b, j = core // 2, core % 2
        own0 = 2048 * j
        xtw = np.zeros((D, T), f32)
        if j == 0:
            xtw[:, 2048:] = x[b, 0:2048].T
        else:
            xtw[:, :] = x[b].T
        pos = (np.arange(T) + own0 - 2048).astype(f32)
        tabs = {}
        for nm, rd in (("tabq", 32), ("tabi", 16)):
            half = rd // 2
            inv = np.power(f32(500000.0), -np.arange(half, dtype=f32) * f32(2.0) / f32(rd)).astype(f32)
            ang = (pos[:, None] * inv[None, :]).astype(f32)
            tab = np.stack([np.cos(ang), np.sin(ang)], axis=1).astype(f32)
            tabs[nm] = np.ascontiguousarray(tab.reshape(32, 128, 2, half).transpose(1, 0, 2, 3))
        m = dict(shared)
        m.update(tabs)
        m["xtw"] = xtw
        m["xown"] = np.ascontiguousarray(x[b, own0:own0 + 2048])
        m["xhalo"] = np.ascontiguousarray(x[b, own0 - 128:own0]) if j == 1 else np.zeros((128, D), f32)
        m["kbias"] = np.full((128, 1), 0.0 if j == 1 else NEG, f32)
        m["flag"] = np.full((128, 1), float(j), f32)
        in_maps.append(m)
    key = bool(_debug)
    if key not in _NC_CACHE:
        _NC_CACHE[key] = build_nc(debug=key)[0]
    nc = _NC_CACHE[key]
    res = run_bass_kernel_spmd(nc, in_maps, core_ids=list(range(8)))
    out = np.zeros((4, T, D), f32)
    for core in range(8):
        b, j = core // 2, core % 2
        out[b, 2048 * j:2048 * (j + 1)] = res.results[core]["out"]
    if _debug:
        return out, res.results
    return out
```
